# Optimizing a Trainium2 kernel written in Bass

```python
import math
import jax, jax.numpy as jnp
from jax import lax
import numpy as np

D_MODEL = 4096
BATCH = 2
SEQ = 8192
DEPTH = 4

POOL_WIDTH = D_MODEL // 2
POOL_WINDOWS = (2, 4, 8, 16)
N_POOL_GROUPS = len(POOL_WINDOWS)
POOL_GROUP = POOL_WIDTH // N_POOL_GROUPS
DIFF_HEAD_DIM = 64
DIFF_WIDTH = D_MODEL - POOL_WIDTH
DIFF_HEADS = DIFF_WIDTH // (2 * DIFF_HEAD_DIM)
EVEN_IN = POOL_WIDTH + 3 * DIFF_WIDTH
EVEN_OUT = POOL_WIDTH + DIFF_WIDTH
SWA_HEADS = 64
SWA_KV_HEADS = 8
SWA_HEAD_DIM = D_MODEL // SWA_HEADS
SWA_GROUP = SWA_HEADS // SWA_KV_HEADS
WINDOW = 128
SWA_Q_WIDTH = SWA_HEADS * SWA_HEAD_DIM
SWA_KV_WIDTH = SWA_KV_HEADS * SWA_HEAD_DIM
ODD_IN = SWA_Q_WIDTH + 2 * SWA_KV_WIDTH
D_FF = 4 * D_MODEL
Q_BLOCK = 128
RMS_EPS = 1e-5
N_EVEN = (DEPTH + 1) // 2
N_ODD = DEPTH // 2

kernel_name = 'hybrid_pool_diffattn_swa_sink_sqrelu'


def rmsnorm(x, g):
    xf = x.astype(jnp.float32)
    y = xf * lax.rsqrt(jnp.mean(xf * xf, axis=-1, keepdims=True) + RMS_EPS)
    return (y * g.astype(jnp.float32)).astype(x.dtype)


def lambda_init_fn(layer_idx):
    return 0.8 - 0.6 * math.exp(-0.3 * layer_idx)


def multiscale_pool(u, w_pool, pool_scale):
    b, s, _ = u.shape
    uf = u.astype(jnp.float32)
    cs = jnp.concatenate([jnp.zeros((b, 1, POOL_WIDTH), jnp.float32),
                          jnp.cumsum(uf, axis=1)], axis=1)
    t = jnp.arange(s)
    outs = []
    for g, w in enumerate(POOL_WINDOWS):
        sl = slice(g * POOL_GROUP, (g + 1) * POOL_GROUP)
        start = jnp.maximum(t + 1 - w, 0)
        win_sum = cs[:, 1:, sl] - cs[:, start, sl]
        count = jnp.minimum(t + 1, w).astype(jnp.float32)[None, :, None]
        outs.append(win_sum / count - uf[:, :, sl])
    p = jnp.stack(outs, axis=2).astype(u.dtype)
    y = jnp.einsum('bsgc,gcd->bsgd', p, w_pool)
    return y.reshape(b, s, POOL_WIDTH) * pool_scale


def diff_attention(q, k, v, lam, subln_g, lambda_init):
    b, s = q.shape[:2]
    nb = s // Q_BLOCK
    scale = DIFF_HEAD_DIM ** -0.5
    q_blocks = q.reshape(b, nb, Q_BLOCK, DIFF_HEADS, 2, DIFF_HEAD_DIM).transpose(1, 0, 2, 3, 4, 5)
    k_pos = jnp.arange(s)

    def block(args):
        qb, n = args
        q_pos = n * Q_BLOCK + jnp.arange(Q_BLOCK)
        sc = jnp.einsum('bqhcd,bkhcd->bhcqk', qb, k).astype(jnp.float32) * scale
        causal = k_pos[None, :] <= q_pos[:, None]
        sc = jnp.where(causal, sc, -jnp.inf)
        a = jax.nn.softmax(sc, axis=-1)
        a = a[:, :, 0] - lam * a[:, :, 1]
        return jnp.einsum('bhqk,bkhe->bqhe', a.astype(v.dtype), v)

    o = lax.map(block, (q_blocks, jnp.arange(nb)))
    o = o.transpose(1, 0, 2, 3, 4).reshape(b, s, DIFF_HEADS, 2 * DIFF_HEAD_DIM)
    o = rmsnorm(o, subln_g) * (1.0 - lambda_init)
    return o.reshape(b, s, DIFF_WIDTH)


def swa_sink_attention(q, k, v, sinks):
    b, s = q.shape[:2]
    nb = s // WINDOW
    qb = q.reshape(b, nb, WINDOW, SWA_KV_HEADS, SWA_GROUP, SWA_HEAD_DIM).transpose(1, 0, 2, 3, 4, 5)

    def banded(t):
        t = t.reshape(b, nb, WINDOW, SWA_KV_HEADS, SWA_HEAD_DIM)
        prev = jnp.pad(t[:, :-1], ((0, 0), (1, 0), (0, 0), (0, 0), (0, 0)))
        return jnp.concatenate([prev, t], axis=2).transpose(1, 0, 2, 3, 4)

    kk, vv = banded(k), banded(v)
    q_rel = WINDOW + jnp.arange(WINDOW)[:, None]
    k_rel = jnp.arange(2 * WINDOW)[None, :]
    band = (k_rel <= q_rel) & (q_rel - k_rel < WINDOW)
    sink_logit = sinks.astype(jnp.float32).reshape(SWA_KV_HEADS, SWA_GROUP)[None, :, :, None, None]
    scale = SWA_HEAD_DIM ** -0.5

    def block(args):
        qi, ki, vi, n = args
        sc = jnp.einsum('bqhgd,bkhd->bhgqk', qi, ki).astype(jnp.float32) * scale
        mask = band & ((n > 0) | (k_rel >= WINDOW))
        sc = jnp.where(mask, sc, -jnp.inf)
        logits = jnp.concatenate([sc, jnp.broadcast_to(sink_logit, sc.shape[:-1] + (1,))], axis=-1)
        p = jax.nn.softmax(logits, axis=-1)[..., :-1]
        return jnp.einsum('bhgqk,bkhd->bqhgd', p.astype(vi.dtype), vi)

    o = lax.map(block, (qb, kk, vv, jnp.arange(nb)))
    return o.transpose(1, 0, 2, 3, 4, 5).reshape(b, s, SWA_Q_WIDTH)


def squared_relu_mlp(h, w_up, w_down):
    a = jax.nn.relu(h @ w_up)
    return (a * a) @ w_down


def setup_inputs(seed: int = 0) -> dict:
    key = jax.random.key(seed)
    ks = jax.random.split(key, 24)
    f = jnp.float32

    def nrm(k, shape, scale):
        return jax.random.normal(k, shape, f) * scale

    return {
        'x': nrm(ks[0], (BATCH, SEQ, D_MODEL), 1.0),
        'norm_mix': 1.0 + nrm(ks[1], (DEPTH, D_MODEL), 0.05),
        'norm_mlp': 1.0 + nrm(ks[2], (DEPTH, D_MODEL), 0.05),
        'norm_final': 1.0 + nrm(ks[3], (D_MODEL,), 0.05),
        'w_in_even': nrm(ks[4], (N_EVEN, D_MODEL, EVEN_IN), D_MODEL ** -0.5),
        'w_pool': nrm(ks[5], (N_EVEN, N_POOL_GROUPS, POOL_GROUP, POOL_GROUP), POOL_GROUP ** -0.5),
        'pool_scale': 1.0 + nrm(ks[6], (N_EVEN, POOL_WIDTH), 0.1),
        'lambda_q1': nrm(ks[7], (N_EVEN, DIFF_HEAD_DIM), 0.1),
        'lambda_k1': nrm(ks[8], (N_EVEN, DIFF_HEAD_DIM), 0.1),
        'lambda_q2': nrm(ks[9], (N_EVEN, DIFF_HEAD_DIM), 0.1),
        'lambda_k2': nrm(ks[10], (N_EVEN, DIFF_HEAD_DIM), 0.1),
        'subln_g': 1.0 + nrm(ks[11], (N_EVEN, 2 * DIFF_HEAD_DIM), 0.05),
        'w_out_even': nrm(ks[12], (N_EVEN, EVEN_OUT, D_MODEL), EVEN_OUT ** -0.5),
        'w_in_odd': nrm(ks[13], (N_ODD, D_MODEL, ODD_IN), D_MODEL ** -0.5),
        'b_in_odd': nrm(ks[14], (N_ODD, ODD_IN), 0.02),
        'sinks': nrm(ks[15], (N_ODD, SWA_HEADS), 0.5),
        'w_out_odd': nrm(ks[16], (N_ODD, SWA_Q_WIDTH, D_MODEL), SWA_Q_WIDTH ** -0.5),
        'b_out_odd': nrm(ks[17], (N_ODD, D_MODEL), 0.02),
        'w_up': nrm(ks[18], (DEPTH, D_MODEL, D_FF), D_MODEL ** -0.5),
        'w_down': nrm(ks[19], (DEPTH, D_FF, D_MODEL), D_FF ** -0.5),
    }


def reference(x, norm_mix, norm_mlp, norm_final, w_in_even, w_pool, pool_scale,
              lambda_q1, lambda_k1, lambda_q2, lambda_k2, subln_g, w_out_even,
              w_in_odd, b_in_odd, sinks, w_out_odd, b_out_odd, w_up, w_down):
    b, s, _ = x.shape
    for i in range(DEPTH):
        h = rmsnorm(x, norm_mix[i])
        if i % 2 == 0:
            j = i // 2
            z = h @ w_in_even[j]
            u = z[..., :POOL_WIDTH]
            q = z[..., POOL_WIDTH:POOL_WIDTH + DIFF_WIDTH].reshape(b, s, DIFF_HEADS, 2, DIFF_HEAD_DIM)
            k = z[..., POOL_WIDTH + DIFF_WIDTH:POOL_WIDTH + 2 * DIFF_WIDTH].reshape(b, s, DIFF_HEADS, 2, DIFF_HEAD_DIM)
            v = z[..., POOL_WIDTH + 2 * DIFF_WIDTH:].reshape(b, s, DIFF_HEADS, 2 * DIFF_HEAD_DIM)
            lam_init = lambda_init_fn(i)
            lam = (jnp.exp(jnp.sum(lambda_q1[j] * lambda_k1[j]).astype(jnp.float32))
                   - jnp.exp(jnp.sum(lambda_q2[j] * lambda_k2[j]).astype(jnp.float32)) + lam_init)
            o_a = multiscale_pool(u, w_pool[j], pool_scale[j])
            o_b = diff_attention(q, k, v, lam, subln_g[j], lam_init)
            x = x + jnp.concatenate([o_a, o_b], axis=-1) @ w_out_even[j]
        else:
            j = i // 2
            z = h @ w_in_odd[j] + b_in_odd[j]
            q = z[..., :SWA_Q_WIDTH].reshape(b, s, SWA_KV_HEADS, SWA_GROUP, SWA_HEAD_DIM)
            k = z[..., SWA_Q_WIDTH:SWA_Q_WIDTH + SWA_KV_WIDTH].reshape(b, s, SWA_KV_HEADS, SWA_HEAD_DIM)
            v = z[..., SWA_Q_WIDTH + SWA_KV_WIDTH:].reshape(b, s, SWA_KV_HEADS, SWA_HEAD_DIM)
            o_c = swa_sink_attention(q, k, v, sinks[j])
            x = x + o_c @ w_out_odd[j] + b_out_odd[j]
        h = rmsnorm(x, norm_mlp[i])
        x = x + squared_relu_mlp(h, w_up[i], w_down[i])
    return rmsnorm(x, norm_final)
```

```python
import math
import numpy as np
import concourse.bass as bass
import concourse.mybir as mybir
from concourse.bass_utils import run_bass_kernel_spmd

F32 = mybir.dt.float32
BF16 = mybir.dt.bfloat16
AF = mybir.ActivationFunctionType
ALU = mybir.AluOpType
AX = mybir.AxisListType
NEG = -30000.0
EPS = 1e-5


class Cfg:
    def __init__(self, D=4096, TL=2048, DEPTH=4):
        self.D = D; self.TL = TL; self.DEPTH = DEPTH
        self.T = 512; self.NT = TL // 512; self.KD = D // 128
        self.PW = D // 2; self.PG = self.PW // 4; self.DW = D - self.PW; self.DH = self.DW // 128
        self.EVEN_IN = self.PW + 3 * self.DW
        self.SH = D // 64; self.KVH = self.SH // 8; self.KVW = self.KVH * 64
        self.DFF = 4 * D; self.HF = self.DFF // 2
        self.NE = (DEPTH + 1) // 2; self.NO = DEPTH // 2
        self.CBK = min(512, 2 * self.KVW)
        self.WSLOT = self.KD * 512
        self.NSP = 8
        self.HE = self.DFF // self.NSP
        self.HEC = self.HE // 128
        self.CBD = min(D, self.WSLOT // self.HEC)
        self.HF = self.HE


class Cnt:
    def __init__(self, nc, name):
        self.h = nc.alloc_semaphore(name); self.v = 0; self.inc = 1


class Ring:
    def __init__(self, b, name, bufs, dma_fill=False):
        self.b = b; self.bufs = bufs; self.depth = len(bufs); self.n = 0
        self.fill = [None] * self.depth
        self.free = [[] for _ in range(self.depth)]
        self.dsem = [b.new_dsem(f"{name}_f{i}") for i in range(self.depth)] if dma_fill else None

    def next(self):
        s = self.n % self.depth; self.n += 1; return s

    def wait_free(self, eng, s):
        for ev in self.free[s]:
            self.b.wait(eng, ev)
        self.free[s] = []


class Builder:
    def __init__(self):
        self.nc = bass.Bass("TRN2", target_bir_lowering=False)
        nc = self.nc
        self.eng = {'pe': nc.tensor, 'act': nc.scalar, 'dve': nc.vector, 'pool': nc.gpsimd, 'sp': nc.sync}
        self.cnt = {e: Cnt(nc, "c_" + e) for e in ('pe', 'act', 'dve', 'pool')}
        self.waited = {}
        self.dsems = []
        self.nd = 0

    def new_dsem(self, name):
        c = Cnt(self.nc, name); c.inc = 16; self.dsems.append(c); return c

    def wait(self, eng, ev):
        if ev is None:
            return
        c, v = ev
        key = (eng, id(c))
        if self.waited.get(key, 0) >= v:
            return
        self.waited[key] = v
        self.eng[eng].wait_ge(c.h, v)

    def mark(self, eng, instr):
        c = self.cnt[eng]
        instr.then_inc(c.h, 1); c.v += 1
        self.waited[(eng, id(c))] = c.v
        if eng != 'pe':
            self.eng[eng].wait_ge(c.h, c.v)
        return (c, c.v)

    def dma(self, q, out, in_, dsem):
        try:
            instr = self.eng[q].dma_start(out=out, in_=in_)
        except Exception:
            print("DMA FAIL", q, out, in_, "nd", self.nd)
            raise
        self.nd += 1
        instr.then_inc(dsem.h, 16); dsem.v += 16
        return (dsem, dsem.v)

    def barrier(self):
        evs = [(c, c.v) for c in self.cnt.values() if c.v > 0] + [(c, c.v) for c in self.dsems if c.v > 0]
        for e in ('sp', 'pool', 'pe', 'act', 'dve'):
            for ev in evs:
                self.wait(e, ev)


def build_program(cfg):
    b = Builder(); nc = b.nc
    D, TL, T, NT, KD = cfg.D, cfg.TL, cfg.T, cfg.NT, cfg.KD
    PW, PG, DW, DH = cfg.PW, cfg.PG, cfg.DW, cfg.DH
    SH, KVH, KVW = cfg.SH, cfg.KVH, cfg.KVW
    DFF, HF, NE, NO, DEPTH = cfg.DFF, cfg.HF, cfg.NE, cfg.NO, cfg.DEPTH
    PGC = PG // 128
    HFC = HF // 128
    CBK, CBD = cfg.CBK, cfg.CBD
    NKB = (2 * KVW) // CBK
    wait, mark, dma = b.wait, b.mark, b.dma
    pe, act, dve, pool, sp = nc.tensor, nc.scalar, nc.vector, nc.gpsimd, nc.sync

    def din(name, shape, dt=F32):
        return nc.dram_tensor(name, list(shape), dt, kind="ExternalInput").ap()

    def dscr(name, shape, dt):
        return nc.dram_tensor(name, list(shape), dt).ap()

    MAXB = 64 * 2 ** 20

    class WSplit:
        def __init__(self, name, lead, nblk, width):
            self.per = max(1, MAXB // (128 * width * 4))
            self.parts = {}
            import itertools
            for idx in itertools.product(*[range(n) for n in lead]):
                lst = []
                for p0 in range(0, nblk, self.per):
                    n_ = min(self.per, nblk - p0)
                    lst.append(din(name + "".join(f"_{i}" for i in idx) + f"_p{p0 // self.per}", [n_, 128, width]))
                self.parts[idx] = lst

        def __getitem__(self, key):
            idx, i = tuple(key[:-1]), key[-1]
            return self.parts[idx][i // self.per][i % self.per]

    xT_in = din("xT", [D, TL])
    outT = nc.dram_tensor("outT", [D, TL], F32, kind="ExternalOutput").ap()
    gvec_d = din("gvec", [128, (2 * DEPTH + 1) * KD])
    w_in_e = WSplit("w_in_e", [NE], cfg.EVEN_IN // 512, KD * 512)
    w_pool = din("w_pool", [NE, 4, 128, PGC * PG])
    w_out_e = WSplit("w_out_e", [NE], D // 512, KD * 512)
    pscale_d = din("pscale", [NE, 128, PW // 128])
    lamv_d = din("lamv", [NE, 4, 64])
    subg_d = din("subg", [NE, 128])
    if NO > 0:
        w_q_o = WSplit("w_q_o", [NO], D // 512, KD * 512)
        w_k_o = din("w_k_o", [NO, NKB, 128, KD * CBK])
        w_v_o = din("w_v_o", [NO, 1, 128, KD * KVW])
        w_out_o = WSplit("w_out_o", [NO], D // 512, KD * 512)
        bq_d = din("bq", [NO, 128, D // 128])
        bk_d = din("bk", [NO, 128, 2 * KVW // 128])
        bv_d = din("bv", [NO, KVW])
        sinks_d = din("sinks", [NO, SH])
        bo_d = din("bo", [NO, 128, KD])
    w_up = WSplit("w_up", [DEPTH], DFF // 512, KD * 512)
    w_down = WSplit("w_down", [DEPTH, cfg.NSP], D // CBD, HFC * CBD)
    cmask_d = din("cmask", [128, 8])
    invcnt_d = din("invcnt", [128, 64])

    xA = dscr("xA", [D, TL], F32)
    xB = dscr("xB", [D, TL], F32)
    uT_s = dscr("uT_s", [PW, TL], F32)
    qT_s = dscr("qT_s", [D, TL], BF16)
    ocT_s = dscr("ocT_s", [D, TL], BF16)
    K_pay = dscr("K_pay", [DW, TL], BF16)
    K_all = dscr("K_all", [4 * DW, TL], BF16)
    V_pay = dscr("V_pay", [TL, DW], BF16)
    V_all = dscr("V_all", [4 * TL, DW], BF16)
    MR = PW + 8
    misc_pay = dscr("misc_pay", [MR, 16], F32)
    misc_all = dscr("misc_all", [4 * MR, 16], F32)
    Ko_s = dscr("Ko_s", [2 * KVW, TL], BF16)
    Vo_s = dscr("Vo_s", [TL, KVW], BF16)
    OH = 2 * KVW + 8
    ho_pay = dscr("ho_pay", [OH, 128], BF16)
    ho_all = dscr("ho_all", [4 * OH, 128], BF16)
    hv_pay = dscr("hv_pay", [128, KVW], BF16)
    hv_all = dscr("hv_all", [4 * 128, KVW], BF16)
    mx_s = dscr("mx_s", [128, 2], F32)
    mo_pay = dscr("mo_pay", [8, 16], F32)
    mo_all = dscr("mo_all", [32, 16], F32)

    def sb(name, shape, dt):
        return nc.alloc_sbuf_tensor(name, list(shape), dt)

    wbufs = [sb(f"wb{i}", [128, cfg.WSLOT], BF16) for i in range(2)]
    wring = Ring(b, "w", wbufs, dma_fill=True)
    xin = Ring(b, "xin", [sb(f"xin{i}", [128, T], F32) for i in range(4)], dma_fill=True)
    ost = Ring(b, "ost", [sb(f"ost{i}", [128, T], F32) for i in range(2)])
    ost_ds = [b.new_dsem(f"ost_s{i}") for i in range(2)]
    osb = Ring(b, "osb", [sb(f"osb{i}", [128, T], BF16) for i in range(3)])
    osb_ds = [b.new_dsem(f"osb_s{i}") for i in range(3)]
    sq = Ring(b, "sq", [sb(f"sq{i}", [128, T], BF16) for i in range(2)])
    gvec = sb("gvec_sb", [128, (2 * DEPTH + 1) * KD], F32)
    cmask = sb("cmask_sb", [128, 8], F32)
    invcnt = sb("invcnt_sb", [128, 64], F32)
    ident = sb("ident", [128, 128], BF16)
    ones_bf = sb("ones_bf", [128, 128], BF16)
    blk_ones = sb("blk_ones", [128, 128], BF16)
    tri_le = sb("tri_le", [128, 128], BF16)
    tri_gt = sb("tri_gt", [128, 128], BF16)
    tmpf = sb("tmpf", [128, 128], F32)
    rstd = sb("rstd", [128, T], F32)
    rtmp = sb("rtmp", [128, T], F32)
    small = sb("small", [128, 64], F32)
    eps_t = sb("eps_t", [128, 1], F32)
    misc_ds = b.new_dsem("misc_ds")
    cc_sem = Cnt(nc, "cc_sem")

    pb = [nc.alloc_psum_tensor(f"pb{i}", [128, 512], F32) for i in range(7)]
    pbt = nc.alloc_psum_tensor("pbt", [128, 1024], BF16)
    pfree = [[] for _ in range(8)]

    def bank_wait(eng, i):
        for ev in pfree[i]:
            wait(eng, ev)
        pfree[i] = []

    e0 = dma('sp', gvec[:], gvec_d, misc_ds)
    e0 = dma('sp', cmask[:], cmask_d, misc_ds)
    e0 = dma('sp', invcnt[:], invcnt_d, misc_ds)
    mark('pool', pool.memset(tmpf[:], 1.0))
    ev = mark('pool', pool.affine_select(out=tmpf[:], in_=tmpf[:], pattern=[[-1, 128]], compare_op=ALU.is_equal,
                                         fill=0.0, base=0, channel_multiplier=1))
    wait('dve', ev)
    ev = mark('dve', dve.tensor_copy(out=ident[:], in_=tmpf[:]))
    wait('pool', ev)
    mark('pool', pool.memset(tmpf[:], 1.0))
    ev = mark('pool', pool.affine_select(out=tmpf[:], in_=tmpf[:], pattern=[[1, 128]], compare_op=ALU.is_ge,
                                         fill=0.0, base=0, channel_multiplier=-1))
    wait('dve', ev)
    mark('dve', dve.tensor_copy(out=tri_le[:], in_=tmpf[:]))
    mark('dve', dve.tensor_scalar(out=tmpf[:], in0=tmpf[:], scalar1=-1.0, scalar2=1.0, op0=ALU.mult, op1=ALU.add))
    mark('dve', dve.tensor_copy(out=tri_gt[:], in_=tmpf[:]))
    mark('dve', dve.memset(ones_bf[:], 1.0))
    mark('dve', dve.memset(blk_ones[:], 0.0))
    mark('dve', dve.memset(blk_ones[0:64, 0:64], 1.0))
    mark('dve', dve.memset(blk_ones[64:128, 64:128], 1.0))
    mark('dve', dve.memset(eps_t[:], EPS))
    b.barrier()

    def load_w(block_ap, ncols):
        s = wring.next()
        wring.wait_free('pool', s)
        wring.fill[s] = dma('pool', wbufs[s][:, 0:ncols], block_ap, wring.dsem[s])
        return s

    class WStream:
        def __init__(self, blocks):
            self.blocks = blocks; self.issued = 0; self.slots = []

        def get(self, i):
            while self.issued < len(self.blocks) and self.issued <= i + 1 and self.issued - i < 2:
                ap, n = self.blocks[self.issued]
                self.slots.append(load_w(ap, n)); self.issued += 1
            return self.slots[i]

    def norm_tile(xsrc, tt, gcol, hT):
        tsl = slice(tt * T, (tt + 1) * T)
        bank_wait('pe', 6)
        e3 = None
        for c in range(KD):
            s = xin.next(); xin.wait_free('sp', s)
            ev = dma('sp', xin.bufs[s][:], xsrc[c * 128:(c + 1) * 128, tsl], xin.dsem[s])
            q = sq.next(); sq.wait_free('act', q); wait('act', ev)
            e2 = mark('act', act.activation(out=sq.bufs[q][:], in_=xin.bufs[s][:], func=AF.Square))
            xin.free[s].append(e2)
            wait('pe', e2)
            e3 = mark('pe', pe.matmul(pb[6][:], lhsT=ones_bf[:], rhs=sq.bufs[q][:], start=(c == 0), stop=(c == KD - 1)))
            sq.free[q].append(e3)
        wait('act', e3)
        e4 = mark('act', act.activation(out=rtmp[:], in_=pb[6][:], func=AF.Sqrt, bias=eps_t[:], scale=1.0 / D))
        pfree[6].append(e4)
        wait('dve', e4)
        e5 = mark('dve', dve.reciprocal(out=rstd[:], in_=rtmp[:]))
        e6 = None
        for c in range(KD):
            s = xin.next(); xin.wait_free('sp', s)
            ev = dma('sp', xin.bufs[s][:], xsrc[c * 128:(c + 1) * 128, tsl], xin.dsem[s])
            wait('dve', ev)
            e6 = mark('dve', dve.scalar_tensor_tensor(out=hT[:, c, :], in0=xin.bufs[s][:], scalar=gvec[:, gcol + c:gcol + c + 1],
                                                      in1=rstd[:], op0=ALU.mult, op1=ALU.mult))
            xin.free[s].append(e6)
        return e6

    lin_bank = [0]

    def lin_fm(inT, KC, in_ev, ws, blk0, nblk, CB, evac, in_free=None):
        last = None
        for bi in range(nblk):
            s = ws.get(blk0 + bi)
            wait('pe', wring.fill[s])
            wait('pe', in_ev)
            for m in range(CB // 128):
                pbi = lin_bank[0] % 4; lin_bank[0] += 1
                bank_wait('pe', pbi)
                for k in range(KC):
                    mm = pe.matmul(pb[pbi][:], lhsT=wbufs[s][:, k * CB + m * 128:k * CB + (m + 1) * 128],
                                   rhs=inT[:, k, :], start=(k == 0), stop=(k == KC - 1))
                rdy = mark('pe', mm)
                fe = evac(bi * (CB // 128) + m, pbi, rdy)
                pfree[pbi].append(fe)
                last = rdy
            wring.free[s].append(last)
        if in_free is not None:
            in_free.append(last)
        return last

    def lin_fm2(inTs, KC, in_evs, ws, blk0, nblk, CB, evacs, in_free=None):
        last = None
        n = len(inTs)
        for bi in range(nblk):
            s = ws.get(blk0 + bi)
            wait('pe', wring.fill[s])
            for ev in in_evs:
                wait('pe', ev)
            for m in range(CB // 128):
                pbis = []
                for _ in range(n):
                    pbi = lin_bank[0] % 4; lin_bank[0] += 1
                    bank_wait('pe', pbi); pbis.append(pbi)
                mms = [None] * n
                for k in range(KC):
                    for t_ in range(n):
                        mms[t_] = pe.matmul(pb[pbis[t_]][:], lhsT=wbufs[s][:, k * CB + m * 128:k * CB + (m + 1) * 128],
                                            rhs=inTs[t_][:, k, :], start=(k == 0), stop=(k == KC - 1))
                rdys = [mark('pe', mm) for mm in mms]
                for t_ in range(n):
                    fe = evacs[t_](bi * (CB // 128) + m, pbis[t_], rdys[t_])
                    pfree[pbis[t_]].append(fe)
                last = rdys[-1]
            wring.free[s].append(last)
        if in_free is not None:
            in_free.append(last)
        return last

    def store(q, dst, ring, ds, s, ev):
        wait(q, ev)
        e = dma(q, dst, ring.bufs[s][:], ds[s])
        ring.free[s].append(e)
        return e

    def evac_store_bf(dst_fn, bias_fn=None, scale_fn=None, eng_alt=True):
        def f(ti, pbi, rdy):
            s = osb.next(); osb.wait_free('act', s)
            wait('act', rdy)
            if bias_fn is not None:
                e = mark('act', act.activation(out=osb.bufs[s][:], in_=pb[pbi][:], func=AF.Identity, bias=bias_fn(ti), scale=1.0))
            elif scale_fn is not None:
                e = mark('act', act.activation(out=osb.bufs[s][:], in_=pb[pbi][:], func=AF.Identity, scale=scale_fn(ti)))
            else:
                e = mark('act', act.activation(out=osb.bufs[s][:], in_=pb[pbi][:], func=AF.Copy))
            store('sp', dst_fn(ti), osb, osb_ds, s, e)
            return e
        return f

    def evac_residual(xsrc, xdst, tt, bias_fn=None):
        tsl = slice(tt * T, (tt + 1) * T)

        def f(ti, pbi, rdy):
            if ti == 0:
                for ds_ in ost_ds:
                    wait('sp', (ds_, ds_.v))
            xs = xin.next(); xin.wait_free('sp', xs)
            ev = dma('sp', xin.bufs[xs][:], xsrc[ti * 128:(ti + 1) * 128, tsl], xin.dsem[xs])
            s = ost.next(); ost.wait_free('dve', s)
            wait('dve', ev); wait('dve', rdy)
            if bias_fn is None:
                e = mark('dve', dve.tensor_tensor(out=ost.bufs[s][:], in0=pb[pbi][:], in1=xin.bufs[xs][:], op=ALU.add))
            else:
                e = mark('dve', dve.scalar_tensor_tensor(out=ost.bufs[s][:], in0=pb[pbi][:], scalar=bias_fn(ti),
                                                         in1=xin.bufs[xs][:], op0=ALU.add, op1=ALU.add))
            xin.free[xs].append(e)
            store('sp', xdst[ti * 128:(ti + 1) * 128, tsl], ost, ost_ds, s, e)
            return e
        return f

    def blocks_of(wd, idx, nblk, ncols):
        return [(wd[idx + (i,)] if isinstance(idx, tuple) else wd[idx, i], ncols) for i in range(nblk)]

    import contextlib
    hones = [sb("hones0", [128, 128], BF16), sb("hones1", [128, 128], BF16)]
    mrep_le = sb("mrep_le", [128, 1024], BF16)
    mrep_gt = sb("mrep_gt", [128, 1024], BF16)
    biasT = sb("biasT", [128, 16], F32)
    mxc = sb("mxc", [128, 64], F32)
    mxb = sb("mxb", [128, 4], BF16)
    gsub = sb("gsub", [128, 128], F32)
    lamt = sb("lamt", [128, 256], F32)
    zt = sb("zt", [128, 512], BF16)
    mark('dve', dve.memset(zt[:], 0.0))
    for c in range(2):
        mark('dve', dve.memset(hones[c][:], 0.0))
        mark('dve', dve.memset(hones[c][c * 64:(c + 1) * 64, :], 1.0))
    for g in range(8):
        mark('dve', dve.tensor_copy(out=mrep_le[:, g * 128:(g + 1) * 128], in_=tri_le[:]))
        mark('dve', dve.tensor_copy(out=mrep_gt[:, g * 128:(g + 1) * 128], in_=tri_gt[:]))
    b.barrier()
    pid_sp = sp.partition_id()
    rkS = [sp.snap((pid_sp + S_ + 1) % 4) for S_ in range(4)]
    prevr = rkS[2]
    NB = TL // 128
    K4 = K_all.rearrange("(r d) t -> r d t", r=4)
    V4 = V_all.rearrange("(r b p) c -> r p b c", r=4, p=128)
    M4 = misc_all.rearrange("(r m) c -> r m c", r=4)
    HO4 = ho_all.rearrange("(r m) c -> r m c", r=4)
    HV4 = hv_all.rearrange("(r m) c -> r m c", r=4)
    Kslot = dscr("Kslot", [4, DW, TL], BF16)
    Vslot = dscr("Vslot", [4, TL, DW], BF16)
    Mprev = dscr("Mprev", [MR, 16], F32)
    HOprev = dscr("HOprev", [OH, 128], BF16)
    HVprev = dscr("HVprev", [128, KVW], BF16)
    V4f = V_all.rearrange("(r t) c -> r t c", r=4)

    def mlp_phase(l, xsrc, xmid, xdst):
        PT = 2 if NT % 2 == 0 else 1
        NSP = cfg.NSP
        with contextlib.ExitStack() as st:
            hTs = [st.enter_context(nc.sbuf_tensor(f"hT_m{l}_{i}", [128, KD, T], BF16)) for i in range(PT)]
            aTs = [st.enter_context(nc.sbuf_tensor(f"aT_m{l}_{i}", [128, HFC, T], BF16)) for i in range(PT)]
            rl = st.enter_context(nc.sbuf_tensor(f"rl_m{l}", [128, 2, T], F32))
            blocks = []
            for _ in range(NT // PT):
                for e_ in range(NSP):
                    blocks += [(w_up[l, e_ * (HF // 512) + i], KD * 512) for i in range(HF // 512)]
                    blocks += [(w_down[l, e_, i], HFC * CBD) for i in range(D // CBD)]
            ws = WStream(blocks)
            bi = 0
            hT_free = []
            aT_free = []
            rl_free = [[], []]
            rl_n = [0]
            for pp in range(NT // PT):
                tts = [pp * PT + i for i in range(PT)]
                for ev in hT_free:
                    wait('dve', ev)
                hT_free = []
                hevs = [norm_tile(xsrc, tts[i], (DEPTH + l) * KD, hTs[i]) for i in range(PT)]
                for e_ in range(NSP):
                    a_last = [None] * PT

                    def mk_evac_up(i_):
                        def evac_up(ti, pbi, rdy):
                            r = rl_n[0] % 2; rl_n[0] += 1
                            for ev in rl_free[r]:
                                wait('act', ev)
                            rl_free[r] = []
                            if ti == 0 and i_ == 0:
                                for ev in aT_free:
                                    wait('dve', ev)
                                aT_free.clear()
                            wait('act', rdy)
                            e1 = mark('act', act.activation(out=rl[:, r, :], in_=pb[pbi][:], func=AF.Relu))
                            wait('dve', e1)
                            e2 = mark('dve', dve.tensor_tensor(out=aTs[i_][:, ti, :], in0=rl[:, r, :], in1=rl[:, r, :], op=ALU.mult))
                            rl_free[r].append(e2)
                            a_last[i_] = e2
                            return e1
                        return evac_up
                    lin_fm2(hTs, KD, hevs, ws, bi, HF // 512, 512, [mk_evac_up(i) for i in range(PT)],
                            in_free=(hT_free if e_ == NSP - 1 else None))
                    bi += HF // 512
                    if e_ == 0:
                        src, dst = xsrc, xmid
                    elif e_ % 2 == 1:
                        src, dst = xmid, xdst
                    else:
                        src, dst = xdst, xmid
                    lin_fm2(aTs, HFC, list(a_last), ws, bi, D // CBD, CBD, [evac_residual(src, dst, tts[i]) for i in range(PT)],
                            in_free=aT_free)
                    bi += D // CBD
            b.barrier()

    def lin_tm(hT, in_ev, s, ncols, tt, dst, col0, bias_tile=None):
        wait('pe', wring.fill[s]); wait('pe', in_ev)
        last = None
        for tb in range(4):
            pbi = lin_bank[0] % 4; lin_bank[0] += 1
            bank_wait('pe', pbi)
            for k in range(KD):
                mm = pe.matmul(pb[pbi][:, 0:ncols], lhsT=hT[:, k, tb * 128:(tb + 1) * 128],
                               rhs=wbufs[s][:, k * ncols:(k + 1) * ncols], start=(k == 0), stop=(k == KD - 1))
            rdy = mark('pe', mm)
            o = osb.next(); osb.wait_free('dve', o)
            wait('dve', rdy)
            if bias_tile is None:
                e = mark('dve', dve.tensor_copy(out=osb.bufs[o][:, 0:ncols], in_=pb[pbi][:, 0:ncols]))
            else:
                e = mark('dve', dve.tensor_tensor(out=osb.bufs[o][:, 0:ncols], in0=pb[pbi][:, 0:ncols], in1=bias_tile, op=ALU.add))
            pfree[pbi].append(e)
            wait('sp', e)
            r0 = tt * T + tb * 128
            osb.free[o].append(dma('sp', dst[r0:r0 + 128, col0:col0 + ncols], osb.bufs[o][:, 0:ncols], osb_ds[o]))
            last = rdy
        wring.free[s].append(last)
        return last

    def evac_store_f32(dst_fn):
        def f(ti, pbi, rdy):
            s = ost.next(); ost.wait_free('act', s)
            wait('act', rdy)
            e = mark('act', act.activation(out=ost.bufs[s][:], in_=pb[pbi][:], func=AF.Copy))
            store('sp', dst_fn(ti), ost, ost_ds, s, e)
            return e
        return f

    def even_inproj(j, l, xsrc):
        PT = 2 if NT % 2 == 0 else 1
        with contextlib.ExitStack() as st:
            hTs = [st.enter_context(nc.sbuf_tensor(f"hT_e{l}_{i}", [128, KD, T], BF16)) for i in range(PT)]
            nbk = cfg.EVEN_IN // 512
            ws = WStream([(w_in_e[j, i], KD * 512) for _ in range(NT // PT) for i in range(nbk)])
            hfree = []
            for pp in range(NT // PT):
                tts = [pp * PT + i for i in range(PT)]
                tsls = [slice(t_ * T, (t_ + 1) * T) for t_ in tts]
                for ev in hfree:
                    wait('dve', ev)
                hfree = []
                hevs = [norm_tile(xsrc, tts[i], l * KD, hTs[i]) for i in range(PT)]
                base = pp * nbk
                nu, nq = PW // 512, DW // 512
                lin_fm2(hTs, KD, hevs, ws, base, nu, 512,
                        [evac_store_f32(lambda ti, tsl=tsl: uT_s[ti * 128:(ti + 1) * 128, tsl]) for tsl in tsls])
                lin_fm2(hTs, KD, hevs, ws, base + nu, nq, 512,
                        [evac_store_bf(lambda ti, tsl=tsl: qT_s[ti * 128:(ti + 1) * 128, tsl]) for tsl in tsls])
                lin_fm2(hTs, KD, hevs, ws, base + nu + nq, nq, 512,
                        [evac_store_bf(lambda ti, tsl=tsl: K_pay[ti * 128:(ti + 1) * 128, tsl]) for tsl in tsls])
                for vb in range(nq):
                    s = ws.get(base + nu + 2 * nq + vb)
                    for i in range(PT):
                        lastv = lin_tm(hTs[i], hevs[i], s, 512, tts[i], V_pay, vb * 512)
                hfree.append(lastv)
            b.barrier()

    def gather_even():
        e = dma('pool', misc_pay[0:PW, :], uT_s[:, TL - 16:TL], misc_ds)
        wait('pool', e)
        rg = [[0, 1, 2, 3], [4, 5, 6, 7]]
        CCB = 512 * 1024
        rc = max(1, min(DW, CCB // (TL * 2)))
        rv = max(1, min(TL, CCB // (DW * 2)))
        nck, nvk = DW // rc, TL // rv
        pairs = [(K_pay[ck * rc:(ck + 1) * rc, :], K_all[ck * 4 * rc:(ck + 1) * 4 * rc, :]) for ck in range(nck)]
        pairs += [(V_pay[ck * rv:(ck + 1) * rv, :], V_all[ck * 4 * rv:(ck + 1) * 4 * rv, :]) for ck in range(nvk)]
        pairs += [(misc_pay, misc_all)]
        for (i_, o_) in pairs:
            pool.collective_compute("AllGather", ALU.bypass, replica_groups=rg, ins=[i_], outs=[o_]).then_inc(cc_sem.h, 1)
            cc_sem.v += 1
        wait('pool', (cc_sem, cc_sem.v))
        ep = mark('pool', pool.memset(tmpf[:, 0:1], 0.0))
        wait('sp', ep)
        K5 = K_all.rearrange("(ck r rc) t -> ck r rc t", ck=nck, r=4)
        V5 = V_all.rearrange("(ck r rv) c -> ck r rv c", ck=nvk, r=4)
        for S in range(4):
            dma('sp', Kslot[S].rearrange("(ck rc) t -> ck rc t", ck=nck),
                K5[:, bass.ds(rkS[S], 1), :, :].rearrange("ck o rc t -> ck (o rc) t"), misc_ds)
            dma('sp', Vslot[S].rearrange("(ck rv) c -> ck rv c", ck=nvk),
                V5[:, bass.ds(rkS[S], 1), :, :].rearrange("ck o rv c -> ck (o rv) c"), misc_ds)
        dma('sp', Mprev, M4[bass.ds(prevr, 1), :, :].rearrange("o m c -> (o m) c"), misc_ds)
        b.barrier()

    def pool_phase(j):
        with contextlib.ExitStack() as st:
            ub = [st.enter_context(nc.sbuf_tensor(f"ub{j}_{i}", [128, 16 + T], F32)) for i in range(2)]
            ubr = Ring(b, f"ubr{j}", ub, dma_fill=True)
            pa = st.enter_context(nc.sbuf_tensor(f"pa{j}", [128, 16 + T], F32))
            pc = st.enter_context(nc.sbuf_tensor(f"pc{j}", [128, 16 + T], F32))
            t16 = st.enter_context(nc.sbuf_tensor(f"t16{j}", [128, 16], F32))
            pTs = [st.enter_context(nc.sbuf_tensor(f"pT{j}_{i}", [128, PGC, T], BF16)) for i in range(2)]
            pTr = Ring(b, f"pTr{j}", pTs)
            ws = WStream([(w_pool[j, gi], PGC * PG) for _ in range(NT) for gi in range(4)])
            for tt in range(NT):
                tsl = slice(tt * T, (tt + 1) * T)
                for gi in range(4):
                    wdw = (2, 4, 8, 16)[gi]
                    ps_ = pTr.next(); pTr.wait_free('dve', ps_)
                    lastp = None
                    for cc in range(PGC):
                        c = gi * PGC + cc
                        rows = slice(c * 128, (c + 1) * 128)
                        u = ubr.next(); ubr.wait_free('sp', u)
                        if tt == 0:
                            dma('sp', ub[u][:, 16:], uT_s[rows, tsl], ubr.dsem[u])
                            ev = dma('sp', ub[u][:, 0:16], Mprev[c * 128:(c + 1) * 128, :], ubr.dsem[u])
                        else:
                            ev = dma('sp', ub[u][:, :], uT_s[rows, tt * T - 16:(tt + 1) * T], ubr.dsem[u])
                        wait('dve', ev)
                        if tt == 0:
                            mark('dve', dve.tensor_scalar(out=ub[u][:, 0:16], in0=ub[u][:, 0:16], scalar1=cmask[:, 4:5], scalar2=None, op0=ALU.mult))
                        cur = ub[u]; Wd = 16 + T
                        d = 1; k = 0
                        while d < wdw:
                            nxt = pa if k % 2 == 0 else pc
                            mark('dve', dve.tensor_tensor(out=nxt[:, d:Wd], in0=cur[:, d:Wd], in1=cur[:, 0:Wd - d], op=ALU.add))
                            cur = nxt; d *= 2; k += 1
                        lastp = mark('dve', dve.scalar_tensor_tensor(out=pTs[ps_][:, cc, :], in0=cur[:, 16:], scalar=1.0 / wdw,
                                                                     in1=ub[u][:, 16:], op0=ALU.mult, op1=ALU.subtract))
                        if tt == 0:
                            mark('dve', dve.tensor_tensor(out=t16[:], in0=cur[:, 16:32], in1=invcnt[:, gi * 16:(gi + 1) * 16], op=ALU.mult))
                            lastp = mark('dve', dve.tensor_tensor(out=pTs[ps_][:, cc, 0:16], in0=t16[:], in1=ub[u][:, 16:32], op=ALU.subtract))
                        ubr.free[u].append(lastp)
                    blk = tt * 4 + gi
                    lin_fm(pTs[ps_], PGC, lastp, ws, blk, 1, PG,
                           evac_store_bf(lambda ti, gi=gi: ocT_s[gi * PG + ti * 128:gi * PG + (ti + 1) * 128, tsl],
                                         scale_fn=lambda ti, gi=gi: pscale_sb[:, gi * PGC + ti:gi * PGC + ti + 1]),
                           in_free=pTr.free[ps_])
            b.barrier()

    pscale_sb = sb("pscale_sb", [128, PW // 128], F32)
    scale_d = 64 ** -0.5

    def diffattn_phase(j, l):
        lam_init = 0.8 - 0.6 * math.exp(-0.3 * l)
        e = dma('sp', lamt[:], lamv_d[j].rearrange("a d -> (a d)").partition_broadcast(128), misc_ds)
        e = dma('sp', gsub[:], subg_d[j].partition_broadcast(128), misc_ds)
        wait('dve', e);
        mark('dve', dve.tensor_tensor(out=lamt[:, 0:64], in0=lamt[:, 0:64], in1=lamt[:, 64:128], op=ALU.mult))
        mark('dve', dve.tensor_tensor(out=lamt[:, 128:192], in0=lamt[:, 128:192], in1=lamt[:, 192:256], op=ALU.mult))
        mark('dve', dve.reduce_sum(out=small[:, 0:1], in_=lamt[:, 0:64], axis=AX.X))
        ev = mark('dve', dve.reduce_sum(out=small[:, 1:2], in_=lamt[:, 128:192], axis=AX.X))
        wait('act', ev)
        ev = mark('act', act.activation(out=small[:, 2:4], in_=small[:, 0:2], func=AF.Exp))
        wait('dve', ev)
        mark('dve', dve.tensor_tensor(out=small[:, 4:5], in0=small[:, 2:3], in1=small[:, 3:4], op=ALU.subtract))
        mark('dve', dve.tensor_scalar(out=small[:, 5:6], in0=small[:, 4:5], scalar1=-1.0, scalar2=-lam_init, op0=ALU.mult, op1=ALU.add))
        mark('dve', dve.tensor_scalar(out=gsub[:], in0=gsub[:], scalar1=1.0 - lam_init, scalar2=None, op0=ALU.mult))
        neglam = small[:, 5:6]
        with contextlib.ExitStack() as st:
            Kh = [st.enter_context(nc.sbuf_tensor(f"Kh{j}_{i}", [128, 4, TL], BF16)) for i in range(2)]
            Vh = [st.enter_context(nc.sbuf_tensor(f"Vh{j}_{i}", [128, 4 * NB, 129], BF16)) for i in range(2)]
            Qh = [st.enter_context(nc.sbuf_tensor(f"Qh{j}_{i}", [128, TL], BF16)) for i in range(2)]
            hr = Ring(b, f"hr{j}", [None, None], dma_fill=True)
            Pb = [st.enter_context(nc.sbuf_tensor(f"P{j}_{i}", [128, 512], BF16)) for i in range(3)]
            Pr = Ring(b, f"Pr{j}", Pb)
            t2 = st.enter_context(nc.sbuf_tensor(f"t2_{j}", [128, 128], F32))
            of = st.enter_context(nc.sbuf_tensor(f"of_{j}", [128, 128], F32))
            junk = st.enter_context(nc.sbuf_tensor(f"junk_{j}", [128, 128], F32))
            onb = st.enter_context(nc.sbuf_tensor(f"onb_{j}", [128, 128], BF16))
            sm2 = st.enter_context(nc.sbuf_tensor(f"sm2_{j}", [128, 16], F32))
            for i in range(2):
                evm = mark('dve', dve.memset(Vh[i][:, :, 128:129], 1.0))
            wait('sp', evm)

            def issue_loads(h):
                s = hr.next(); hr.wait_free('sp', s)
                for S in range(4):
                    dma('sp', Kh[s][:, S, :], Kslot[S, h * 128:(h + 1) * 128, :], hr.dsem[s])
                    for b0 in range(0, NB, 8):
                        nb_ = min(8, NB - b0)
                        dma('sp', Vh[s][:, S * NB + b0:S * NB + b0 + nb_, 0:128],
                            Vslot[S, b0 * 128:(b0 + nb_) * 128, h * 128:(h + 1) * 128].rearrange("(b p) c -> p b c", p=128), hr.dsem[s])
                hr.fill[s] = dma('sp', Qh[s][:], qT_s[h * 128:(h + 1) * 128, :], hr.dsem[s])
                return s
            slots = {0: issue_loads(0)}
            sbank = [0]
            o_free = []
            for h in range(DH):
                if h + 1 < DH:
                    slots[h + 1] = issue_loads(h + 1)
                s = slots[h]
                ldev = hr.fill[s]
                ncol = 0
                for (src, n512) in [(Qh[s], TL // 512)] + [(Kh[s][:, S, :], TL // 512) for S in range(4)]:
                    for tq in range(n512):
                        q = sq.next(); sq.wait_free('act', q); wait('act', ldev)
                        e2 = mark('act', act.activation(out=sq.bufs[q][:], in_=src[:, tq * 512:(tq + 1) * 512], func=AF.Square))
                        bank_wait('pe', 6); wait('pe', e2)
                        e3 = mark('pe', pe.matmul(pb[6][:], lhsT=blk_ones[:], rhs=sq.bufs[q][:], start=True, stop=True))
                        sq.free[q].append(e3)
                        wait('dve', e3)
                        e4 = mark('dve', dve.reduce_max(out=mxc[:, ncol:ncol + 1], in_=pb[6][:], axis=AX.X))
                        pfree[6].append(e4)
                        ncol += 1
                nq_ = TL // 512
                mark('dve', dve.reduce_max(out=sm2[:, 0:1], in_=mxc[:, 0:nq_], axis=AX.X))
                mark('dve', dve.reduce_max(out=sm2[:, 1:2], in_=mxc[:, nq_:ncol], axis=AX.X))
                mark('dve', dve.tensor_tensor(out=sm2[:, 2:3], in0=sm2[:, 0:1], in1=sm2[:, 1:2], op=ALU.mult))
                e5 = mark('dve', dve.tensor_copy(out=mxb[:, 0:1], in_=sm2[:, 2:3]))
                bank_wait('pe', 6); wait('pe', e5)
                pe.matmul(pb[6][:, 0:1], lhsT=hones[0][:], rhs=mxb[:, 0:1], start=True, stop=True)
                e6 = mark('pe', pe.matmul(pb[6][:, 1:2], lhsT=hones[1][:], rhs=mxb[:, 0:1], start=True, stop=True))
                wait('act', e6)
                e7 = mark('act', act.activation(out=sm2[:, 4:6], in_=pb[6][:, 0:2], func=AF.Sqrt, scale=1.0 / 64))
                pfree[6].append(e7)
                wait('dve', e7)
                mark('dve', dve.tensor_scalar(out=sm2[:, 6:8], in0=sm2[:, 4:6], scalar1=-scale_d * 1.03, scalar2=-0.05, op0=ALU.mult, op1=ALU.add))
                for c in range(2):
                    for S in range(3):
                        mark('dve', dve.tensor_tensor(out=biasT[:, c * 4 + S:c * 4 + S + 1], in0=sm2[:, 6 + c:7 + c], in1=cmask[:, S:S + 1], op=ALU.add))
                    eb = mark('dve', dve.tensor_copy(out=biasT[:, c * 4 + 3:c * 4 + 4], in_=sm2[:, 6 + c:7 + c]))
                wait('act', eb)
                for i in range(NT):
                    nkb = 3 * NB + (i + 1) * 4
                    for ev in o_free:
                        wait('pe', ev)
                    o_free = []
                    lastpv = None
                    for zb in (3, 4, 5):
                        pe.matmul(pb[zb][:], lhsT=zt[:, 0:128], rhs=zt[:], start=True, stop=True)
                    for kb in range(nkb):
                        S = min(kb // NB, 3)
                        kbl = kb - S * NB
                        diag = (S == 3 and kbl >= 4 * i)
                        jb = kbl - 4 * i
                        for c in range(2):
                            bk = sbank[0] % 3; sbank[0] += 1
                            bank_wait('pe', bk); wait('pe', ldev)
                            es = mark('pe', pe.matmul(pb[bk][:], lhsT=Kh[s][c * 64:(c + 1) * 64, S, kbl * 128:(kbl + 1) * 128],
                                                      rhs=Qh[s][c * 64:(c + 1) * 64, i * 512:(i + 1) * 512], start=True, stop=True))
                            p = Pr.next(); Pr.wait_free('act', p); wait('act', es)
                            ee = mark('act', act.activation(out=Pb[p][:], in_=pb[bk][:], func=AF.Exp,
                                                            bias=biasT[:, c * 4 + S:c * 4 + S + 1], scale=scale_d))
                            pfree[bk].append(ee)
                            if diag:
                                wait('dve', ee)
                                ee = mark('dve', dve.tensor_tensor(out=Pb[p][:, jb * 128:(jb + 1) * 128], in0=Pb[p][:, jb * 128:(jb + 1) * 128],
                                                                   in1=tri_le[:], op=ALU.mult))
                            wait('pe', ee)
                            for sbq in range(jb if diag else 0, 4):
                                a = c * 4 + sbq
                                oacc = pb[3 + a // 3][:, (a % 3) * 129:(a % 3) * 129 + 129]
                                mm = pe.matmul(oacc, lhsT=Pb[p][:, sbq * 128:(sbq + 1) * 128], rhs=Vh[s][:, S * NB + kbl, :],
                                               start=False, stop=(diag and jb == sbq))
                            lastpv = mark('pe', mm)
                            Pr.free[p].append(lastpv)
                    wait('dve', lastpv)
                    bank_wait('pe', 7)
                    for sbq in range(4):
                        a1, a2 = sbq, 4 + sbq
                        O1 = pb[3 + a1 // 3][:, (a1 % 3) * 129:(a1 % 3) * 129 + 129]
                        O2 = pb[3 + a2 // 3][:, (a2 % 3) * 129:(a2 % 3) * 129 + 129]
                        mark('dve', dve.memset(sm2[:, 11:12], 0.0))
                        mark('dve', dve.reciprocal(out=sm2[:, 8:9], in_=O1[:, 128:129]))
                        mark('dve', dve.reciprocal(out=sm2[:, 9:10], in_=O2[:, 128:129]))
                        mark('dve', dve.tensor_tensor(out=sm2[:, 10:11], in0=sm2[:, 9:10], in1=neglam, op=ALU.mult))
                        mark('dve', dve.tensor_scalar(out=t2[:], in0=O2[:, 0:128], scalar1=sm2[:, 10:11], scalar2=None, op0=ALU.mult))
                        ev = mark('dve', dve.scalar_tensor_tensor(out=of[:], in0=O1[:, 0:128], scalar=sm2[:, 8:9], in1=t2[:],
                                                                  op0=ALU.mult, op1=ALU.add))
                        if sbq == 3:
                            o_free.append(ev)
                        wait('act', ev)
                        mark('act', act.activation(out=junk[:], in_=of[:], func=AF.Square, accum_out=sm2[:, 11:12]))
                        ev = mark('act', act.activation(out=sm2[:, 12:13], in_=sm2[:, 11:12], func=AF.Sqrt, bias=eps_t[:], scale=1.0 / 128))
                        wait('dve', ev)
                        mark('dve', dve.reciprocal(out=sm2[:, 13:14], in_=sm2[:, 12:13]))
                        ev = mark('dve', dve.scalar_tensor_tensor(out=onb[:], in0=of[:], scalar=sm2[:, 13:14], in1=gsub[:],
                                                                  op0=ALU.mult, op1=ALU.mult))
                        wait('pe', ev)
                        ev = mark('pe', pe.transpose(pbt[:, sbq * 128:(sbq + 1) * 128], onb[:], ident[:]))
                        wait('dve', ev)
                    o = osb.next(); osb.wait_free('act', o)
                    wait('act', ev)
                    ev = mark('act', act.activation(out=osb.bufs[o][:], in_=pbt[:, 0:512], func=AF.Copy))
                    pfree[7].append(ev)
                    store('sp', ocT_s[PW + h * 128:PW + (h + 1) * 128, i * 512:(i + 1) * 512], osb, osb_ds, o, ev)
                hr.free[s].append(lastpv)
            b.barrier()

    def outproj_phase(wd, j, xsrc, xdst, bias_sb=None):
        PT = 2 if NT % 2 == 0 else 1
        with contextlib.ExitStack() as st:
            its = [st.enter_context(nc.sbuf_tensor(f"oin{id(wd) % 997}_{j}_{i}", [128, KD, T], BF16)) for i in range(2)]
            ir = Ring(b, f"ir{id(wd) % 997}_{j}", its, dma_fill=True)
            ws = WStream([(wd[j, i], KD * 512) for _ in range(NT // PT) for i in range(D // 512)])
            bf = (lambda ti: bias_sb[:, ti:ti + 1]) if bias_sb is not None else None
            for pp in range(NT // PT):
                tts = [pp * PT + i for i in range(PT)]
                ss, evs = [], []
                for t_ in tts:
                    s_ = ir.next(); ir.wait_free('sp', s_)
                    evs.append(dma('sp', its[s_][:], ocT_s[:, t_ * T:(t_ + 1) * T].rearrange("(k p) t -> p k t", p=128), ir.dsem[s_]))
                    ss.append(s_)
                fr = []
                lin_fm2([its[s_] for s_ in ss], KD, evs, ws, pp * (D // 512), D // 512, 512,
                        [evac_residual(xsrc, xdst, t_, bias_fn=bf) for t_ in tts], in_free=fr)
                for s_ in ss:
                    ir.free[s_] += fr
            b.barrier()

    if NO > 0:
        bq_sb = sb("bq_sb", [128, D // 128], F32)
        bk_sb = sb("bk_sb", [128, 2 * KVW // 128], F32)
        bv_sb = sb("bv_sb", [128, KVW], F32)
        bo_sb = sb("bo_sb", [128, KD], F32)
        sink_sb = sb("sink_sb", [128, SH], F32)
        exps = sb("exps", [128, 8], F32)

    def odd_inproj(j, l, xsrc):
        dma('sp', bq_sb[:], bq_d[j], misc_ds)
        dma('sp', bk_sb[:], bk_d[j], misc_ds)
        dma('sp', bo_sb[:], bo_d[j], misc_ds)
        dma('sp', sink_sb[:], sinks_d[j].partition_broadcast(128), misc_ds)
        e = dma('sp', bv_sb[:], bv_d[j].partition_broadcast(128), misc_ds)
        for q_ in ('act', 'dve'):
            wait(q_, e)
        with contextlib.ExitStack() as st:
            PT = 2 if NT % 2 == 0 else 1
            hTs = [st.enter_context(nc.sbuf_tensor(f"hT_o{l}_{i}", [128, KD, T], BF16)) for i in range(PT)]
            nqb = D // 512
            blocks = []
            for _ in range(NT // PT):
                blocks += [(w_q_o[j, i], KD * 512) for i in range(nqb)]
                blocks += [(w_k_o[j, i], KD * CBK) for i in range(NKB)]
                blocks += [(w_v_o[j, 0], KD * KVW)]
            ws = WStream(blocks)
            per = nqb + NKB + 1
            hfree = []
            for pp in range(NT // PT):
                tts = [pp * PT + i for i in range(PT)]
                tsls = [slice(t_ * T, (t_ + 1) * T) for t_ in tts]
                for ev in hfree:
                    wait('dve', ev)
                hfree = []
                hevs = [norm_tile(xsrc, tts[i], l * KD, hTs[i]) for i in range(PT)]
                base = pp * per
                lin_fm2(hTs, KD, hevs, ws, base, nqb, 512,
                        [evac_store_bf(lambda ti, tsl=tsl: qT_s[ti * 128:(ti + 1) * 128, tsl], bias_fn=lambda ti: bq_sb[:, ti:ti + 1]) for tsl in tsls])
                lin_fm2(hTs, KD, hevs, ws, base + nqb, NKB, CBK,
                        [evac_store_bf(lambda ti, tsl=tsl: Ko_s[ti * 128:(ti + 1) * 128, tsl], bias_fn=lambda ti: bk_sb[:, ti:ti + 1]) for tsl in tsls])
                s = ws.get(base + nqb + NKB)
                for i in range(PT):
                    lastv = lin_tm(hTs[i], hevs[i], s, KVW, tts[i], Vo_s, 0, bias_tile=bv_sb[:])
                hfree.append(lastv)
            b.barrier()

    def gather_odd():
        dma('pool', ho_pay[0:2 * KVW, :], Ko_s[:, TL - 128:TL], misc_ds)
        e = dma('pool', hv_pay, Vo_s[TL - 128:TL, :], misc_ds)
        wait('pool', e)
        rg = [[0, 1, 2, 3], [4, 5, 6, 7]]
        for (i_, o_) in ((ho_pay, ho_all), (hv_pay, hv_all)):
            pool.collective_compute("AllGather", ALU.bypass, replica_groups=rg, ins=[i_], outs=[o_]).then_inc(cc_sem.h, 1)
            cc_sem.v += 1
        wait('pool', (cc_sem, cc_sem.v))
        ep = mark('pool', pool.memset(tmpf[:, 0:1], 0.0))
        wait('sp', ep)
        dma('sp', HOprev, HO4[bass.ds(prevr, 1), :, :].rearrange("o m c -> (o m) c"), misc_ds)
        dma('sp', HVprev, HV4[bass.ds(prevr, 1), :, :].rearrange("o m c -> (o m) c"), misc_ds)
        b.barrier()

    def swa_phase(j):
        with contextlib.ExitStack() as st:
            Kd = [st.enter_context(nc.sbuf_tensor(f"Kd{j}_{i}", [128, 128 + TL], BF16)) for i in range(2)]
            Vd = [st.enter_context(nc.sbuf_tensor(f"Vd{j}_{i}", [128, NB + 1, 65], BF16)) for i in range(2)]
            Qo = [st.enter_context(nc.sbuf_tensor(f"Qo{j}_{i}", [128, 4, TL], BF16)) for i in range(2)]
            hr = Ring(b, f"hro{j}", [None, None], dma_fill=True)
            Pb = [st.enter_context(nc.sbuf_tensor(f"Po{j}_{i}", [128, 1024], BF16)) for i in range(3)]
            Pr = Ring(b, f"Pro{j}", Pb)
            onb = st.enter_context(nc.sbuf_tensor(f"onbo_{j}", [128, 512], BF16))
            sm2 = st.enter_context(nc.sbuf_tensor(f"sm2o_{j}", [128, 32], F32))
            for i in range(2):
                evm = mark('dve', dve.memset(Vd[i][:, :, 64:65], 1.0))
            wait('sp', evm)

            def issue_loads(kv):
                s = hr.next(); hr.wait_free('sp', s)
                dma('sp', Kd[s][:, 128:], Ko_s[kv * 128:(kv + 1) * 128, :], hr.dsem[s])
                dma('sp', Kd[s][:, 0:128], HOprev[kv * 128:(kv + 1) * 128, :], hr.dsem[s])
                for b0 in range(0, NB, 8):
                    nb_ = min(8, NB - b0)
                    dma('sp', Vd[s][:, 1 + b0:1 + b0 + nb_, 0:64],
                        Vo_s[b0 * 128:(b0 + nb_) * 128, kv * 64:(kv + 1) * 64].rearrange("(b p) c -> p b c", p=128), hr.dsem[s])
                dma('sp', Vd[s][:, 0, 0:64], HVprev[:, kv * 64:(kv + 1) * 64], hr.dsem[s])
                hr.fill[s] = dma('sp', Qo[s][:], qT_s[kv * 512:(kv + 1) * 512, :].rearrange("(t p) n -> p t n", p=128), hr.dsem[s])
                return s
            slots = {0: issue_loads(0)}
            sb2 = [0]
            o_free = []
            KS = int(os.environ.get("KSWA", "99"))
            if KS <= 1:
                b.barrier(); return
            for kv in range(KVH):
                if kv + 1 < KVH:
                    slots[kv + 1] = issue_loads(kv + 1)
                s = slots[kv]
                ldev = hr.fill[s]
                ncol = 0
                srcs = [(Qo[s][:, t, :], TL // 512) for t in range(4)] + [(Kd[s][:, 128:], TL // 512)]
                for (src, n512) in srcs:
                    for tq in range(n512):
                        q = sq.next(); sq.wait_free('act', q); wait('act', ldev)
                        e2 = mark('act', act.activation(out=sq.bufs[q][:], in_=src[:, tq * 512:(tq + 1) * 512], func=AF.Square))
                        bank_wait('pe', 6); wait('pe', e2)
                        e3 = mark('pe', pe.matmul(pb[6][:], lhsT=blk_ones[:], rhs=sq.bufs[q][:], start=True, stop=True))
                        sq.free[q].append(e3)
                        wait('dve', e3)
                        e4 = mark('dve', dve.reduce_max(out=mxc[:, ncol:ncol + 1], in_=pb[6][:], axis=AX.X))
                        pfree[6].append(e4)
                        ncol += 1
                q = sq.next(); sq.wait_free('act', q)
                e2 = mark('act', act.activation(out=sq.bufs[q][:, 0:128], in_=Kd[s][:, 0:128], func=AF.Square))
                bank_wait('pe', 6); wait('pe', e2)
                e3 = mark('pe', pe.matmul(pb[6][:, 0:128], lhsT=blk_ones[:], rhs=sq.bufs[q][:, 0:128], start=True, stop=True))
                sq.free[q].append(e3)
                wait('dve', e3)
                e4 = mark('dve', dve.reduce_max(out=mxc[:, ncol:ncol + 1], in_=pb[6][:, 0:128], axis=AX.X))
                pfree[6].append(e4)
                ncol += 1
                nq_ = 4 * (TL // 512)
                mark('dve', dve.reduce_max(out=sm2[:, 0:1], in_=mxc[:, 0:nq_], axis=AX.X))
                mark('dve', dve.reduce_max(out=sm2[:, 1:2], in_=mxc[:, nq_:ncol], axis=AX.X))
                e5 = mark('dve', dve.tensor_copy(out=mxb[:, 0:2], in_=sm2[:, 0:2]))
                bank_wait('pe', 6); wait('pe', e5)
                e6 = mark('pe', pe.matmul(pb[6][:, 0:2], lhsT=ones_bf[:], rhs=mxb[:, 0:2], start=True, stop=True))
                wait('dve', e6)
                e7 = mark('dve', dve.tensor_copy(out=sm2[:, 6:8], in_=pb[6][:, 0:2]))
                pfree[6].append(e7)
                e7 = mark('dve', dve.tensor_tensor(out=sm2[:, 2:3], in0=sm2[:, 6:7], in1=sm2[:, 7:8], op=ALU.mult))
                wait('act', e7)
                e8 = mark('act', act.activation(out=sm2[:, 3:4], in_=sm2[:, 2:3], func=AF.Sqrt, scale=1.0 / (64 * 128)))
                wait('dve', e8)
                mark('dve', dve.tensor_scalar(out=sm2[:, 4:5], in0=sm2[:, 3:4], scalar1=-scale_d * 1.03, scalar2=-0.05, op0=ALU.mult, op1=ALU.add))
                eb = mark('dve', dve.tensor_tensor(out=sm2[:, 5:6], in0=sm2[:, 4:5], in1=cmask[:, 3:4], op=ALU.add))
                wait('act', eb)
                ex = mark('act', act.activation(out=exps[:], in_=sink_sb[:, kv * 8:(kv + 1) * 8], func=AF.Exp, bias=sm2[:, 4:5], scale=1.0))
                if KS <= 2:
                    b.barrier(); return
                for qb in range(NB):
                    for ev in o_free:
                        wait('pe', ev)
                    o_free = []
                    lastpv = None
                    for zb in (4, 5):
                        pe.matmul(pb[zb][:], lhsT=zt[:, 0:128], rhs=zt[:], start=True, stop=True)
                    for kbi in range(2):
                        kc0 = (qb + kbi) * 128
                        vblk = qb + kbi
                        mrep = mrep_gt if kbi == 0 else mrep_le
                        bias_ap = (sm2[:, 5:6] if qb == 0 else sm2[:, 4:5]) if kbi == 0 else sm2[:, 4:5]
                        bk0 = (sb2[0] % 2) * 2; sb2[0] += 1
                        bank_wait('pe', bk0); bank_wait('pe', bk0 + 1); wait('pe', ldev)
                        for g in range(8):
                            t_, half = g // 2, g % 2
                            mm = pe.matmul(pb[bk0 + half][:, t_ * 128:(t_ + 1) * 128],
                                           lhsT=Kd[s][half * 64:(half + 1) * 64, kc0:kc0 + 128],
                                           rhs=Qo[s][half * 64:(half + 1) * 64, t_, qb * 128:(qb + 1) * 128], start=True, stop=True)
                        es = mark('pe', mm)
                        KSUB = int(os.environ.get("KSUB", "9"))
                        if KSUB <= 0:
                            pfree[bk0].append(es); continue
                        p = Pr.next(); Pr.wait_free('act', p); wait('act', es)
                        e1 = mark('act', act.activation(out=Pb[p][:, 0:512], in_=pb[bk0][:], func=AF.Exp, bias=bias_ap, scale=scale_d))
                        pfree[bk0].append(e1)
                        e2 = mark('act', act.activation(out=Pb[p][:, 512:1024], in_=pb[bk0 + 1][:], func=AF.Exp, bias=bias_ap, scale=scale_d))
                        pfree[bk0 + 1].append(e2)
                        if KSUB <= 1:
                            continue
                        wait('dve', e2)
                        e3 = mark('dve', dve.tensor_tensor(out=Pb[p][:], in0=Pb[p][:], in1=mrep[:], op=ALU.mult))
                        wait('pe', e3)
                        if KS <= 3:
                            continue
                        for g in range(8):
                            pix = (g % 2) * 4 + g // 2
                            mm = pe.matmul(pb[4 + g // 4][:, (g % 4) * 65:(g % 4) * 65 + 65], lhsT=Pb[p][:, pix * 128:(pix + 1) * 128],
                                           rhs=Vd[s][:, vblk, :], start=False, stop=(kbi == 1))
                        lastpv = mark('pe', mm)
                        Pr.free[p].append(lastpv)
                    if KS <= 4:
                        b.barrier(); return
                    wait('dve', lastpv); wait('dve', ex)
                    bank_wait('pe', 7)
                    for g in range(8):
                        O = pb[4 + g // 4][:, (g % 4) * 65:(g % 4) * 65 + 65]
                        mark('dve', dve.tensor_tensor(out=sm2[:, 8 + g:9 + g], in0=O[:, 64:65], in1=exps[:, g:g + 1], op=ALU.add))
                        mark('dve', dve.reciprocal(out=sm2[:, 16 + g:17 + g], in_=sm2[:, 8 + g:9 + g]))
                        ev = mark('dve', dve.tensor_scalar(out=onb[:, g * 64:(g + 1) * 64], in0=O[:, 0:64], scalar1=sm2[:, 16 + g:17 + g],
                                                           scalar2=None, op0=ALU.mult))
                    o_free.append(ev)
                    if KS <= 5:
                        b.barrier(); return
                    wait('pe', ev)
                    for t_ in range(4):
                        ev = mark('pe', pe.transpose(pbt[:, t_ * 128:(t_ + 1) * 128], onb[:, t_ * 128:(t_ + 1) * 128], ident[:]))
                    wait('dve', ev)
                    o = osb.next(); osb.wait_free('act', o)
                    wait('act', ev)
                    ev = mark('act', act.activation(out=osb.bufs[o][:], in_=pbt[:, 0:512], func=AF.Copy))
                    pfree[7].append(ev)
                    wait('sp', ev)
                    osb.free[o].append(dma('sp', ocT_s[kv * 512:(kv + 1) * 512, qb * 128:(qb + 1) * 128].rearrange("(t p) n -> p t n", p=128),
                                           osb.bufs[o][:].rearrange("p (t n) -> p t n", t=4), osb_ds[o]))
                    if KS <= 6:
                        b.barrier(); return
                hr.free[s].append(lastpv)
            b.barrier()

    def final_norm(xsrc):
        gcol = 2 * DEPTH * KD
        for tt in range(NT):
            tsl = slice(tt * T, (tt + 1) * T)
            bank_wait('pe', 6)
            for c in range(KD):
                s = xin.next(); xin.wait_free('sp', s)
                ev = dma('sp', xin.bufs[s][:], xsrc[c * 128:(c + 1) * 128, tsl], xin.dsem[s])
                q = sq.next(); sq.wait_free('act', q); wait('act', ev)
                e2 = mark('act', act.activation(out=sq.bufs[q][:], in_=xin.bufs[s][:], func=AF.Square))
                xin.free[s].append(e2)
                wait('pe', e2)
                e3 = mark('pe', pe.matmul(pb[6][:], lhsT=ones_bf[:], rhs=sq.bufs[q][:], start=(c == 0), stop=(c == KD - 1)))
                sq.free[q].append(e3)
            wait('act', e3)
            e4 = mark('act', act.activation(out=rtmp[:], in_=pb[6][:], func=AF.Sqrt, bias=eps_t[:], scale=1.0 / D))
            pfree[6].append(e4)
            wait('dve', e4)
            mark('dve', dve.reciprocal(out=rstd[:], in_=rtmp[:]))
            for c in range(KD):
                s = xin.next(); xin.wait_free('sp', s)
                ev = dma('sp', xin.bufs[s][:], xsrc[c * 128:(c + 1) * 128, tsl], xin.dsem[s])
                o = ost.next(); ost.wait_free('dve', o)
                wait('dve', ev)
                e6 = mark('dve', dve.scalar_tensor_tensor(out=ost.bufs[o][:], in0=xin.bufs[s][:], scalar=gvec[:, gcol + c:gcol + c + 1],
                                                          in1=rstd[:], op0=ALU.mult, op1=ALU.mult))
                xin.free[s].append(e6)
                store('sp', outT[c * 128:(c + 1) * 128, tsl], ost, ost_ds, o, e6)
        b.barrier()

    import os
    KP = int(os.environ.get("KPHASES", "9999"))
    pcount = [0]

    def go():
        pcount[0] += 1
        return pcount[0] <= KP
    bufs = [xA, xB]
    cur = xT_in
    nxt_i = 0
    for l in range(DEPTH):
        j = l // 2
        mixo = bufs[nxt_i]; nxt_i ^= 1
        if l % 2 == 0:
            e = dma('sp', pscale_sb[:], pscale_d[j], misc_ds)
            wait('act', e)
            if go(): even_inproj(j, l, cur)
            if go(): gather_even()
            if go(): pool_phase(j)
            if go(): diffattn_phase(j, l)
            if go():
                outproj_phase(w_out_e, j, cur, mixo)
                cur = mixo
        else:
            if go(): odd_inproj(j, l, cur)
            if go(): gather_odd()
            if go(): swa_phase(j)
            if go():
                outproj_phase(w_out_o, j, cur, mixo, bias_sb=bo_sb)
                cur = mixo
        mid = bufs[nxt_i]
        if go():
            mlp_phase(l, mixo, mid, mixo)
            cur = mixo
    final_norm(cur)
    if os.environ.get("KDEBUG"):
        for nm, src, shp, dt_ in (("dbg_oc", ocT_s, [D, TL], BF16), ("dbg_q", qT_s, [D, TL], BF16), ("dbg_u", uT_s, [PW, TL], F32),
                                  ("dbg_x", xA, [D, TL], F32)):
            dd = nc.dram_tensor(nm, shp, dt_, kind="ExternalOutput").ap()
            e = dma('sp', dd, src, misc_ds)
        wait('sp', e)
    return nc


def _blk(w, CB):
    K, N = w.shape
    return np.ascontiguousarray(w.reshape(K // 128, 128, N // CB, CB).transpose(2, 1, 0, 3)).reshape(N // CB, 128, (K // 128) * CB)


def _pp(v):
    return np.ascontiguousarray(v.reshape(-1, 128).T)


def _split_host(com, name, arr, nlead):
    import itertools
    lead = arr.shape[:nlead]
    nblk, _, width = arr.shape[nlead:]
    per = max(1, (64 * 2 ** 20) // (128 * width * 4))
    for idx in itertools.product(*[range(n) for n in lead]):
        a = arr[idx]
        for p0 in range(0, nblk, per):
            com[name + "".join(f"_{i}" for i in idx) + f"_p{p0 // per}"] = np.ascontiguousarray(a[p0:p0 + per])


def make_inputs(cfg, inp):
    D, TL, DEPTH = cfg.D, cfg.TL, cfg.DEPTH
    PW, DW, KVW, KVH = cfg.PW, cfg.DW, cfg.KVW, cfg.KVH
    f = np.float32
    x = np.asarray(inp['x'], f)
    Bn, S, _ = x.shape
    com = {}
    g = [_pp(np.asarray(inp['norm_mix'], f)[l]) for l in range(DEPTH)] + [_pp(np.asarray(inp['norm_mlp'], f)[l]) for l in range(DEPTH)] \
        + [_pp(np.asarray(inp['norm_final'], f))]
    com['gvec'] = np.ascontiguousarray(np.concatenate(g, axis=1))
    wie = np.asarray(inp['w_in_even'], f)
    _split_host(com, 'w_in_e', np.stack([_blk(wie[j], 512) for j in range(cfg.NE)]), 1)
    wp = np.asarray(inp['w_pool'], f)
    com['w_pool'] = np.stack([np.stack([_blk(wp[j, gi], cfg.PG)[0] for gi in range(4)]) for j in range(cfg.NE)])
    _split_host(com, 'w_out_e', np.stack([_blk(np.asarray(inp['w_out_even'], f)[j], 512) for j in range(cfg.NE)]), 1)
    com['pscale'] = np.stack([_pp(np.asarray(inp['pool_scale'], f)[j]) for j in range(cfg.NE)])
    com['lamv'] = np.ascontiguousarray(np.stack([np.asarray(inp[k], f) for k in ('lambda_q1', 'lambda_k1', 'lambda_q2', 'lambda_k2')], axis=1))
    com['subg'] = np.asarray(inp['subln_g'], f)
    if cfg.NO > 0:
        wio = np.asarray(inp['w_in_odd'], f); bio = np.asarray(inp['b_in_odd'], f)
        _split_host(com, 'w_q_o', np.stack([_blk(wio[j][:, :D], 512) for j in range(cfg.NO)]), 1)
        dup = np.concatenate([np.arange(D + h * 64, D + (h + 1) * 64) for h in range(KVH) for _ in range(2)])
        com['w_k_o'] = np.stack([_blk(wio[j][:, dup], cfg.CBK) for j in range(cfg.NO)])
        com['w_v_o'] = np.stack([_blk(wio[j][:, D + KVW:], KVW) for j in range(cfg.NO)])
        _split_host(com, 'w_out_o', np.stack([_blk(np.asarray(inp['w_out_odd'], f)[j], 512) for j in range(cfg.NO)]), 1)
        com['bq'] = np.stack([_pp(bio[j][:D]) for j in range(cfg.NO)])
        com['bk'] = np.stack([_pp(bio[j][dup]) for j in range(cfg.NO)])
        com['bv'] = np.ascontiguousarray(bio[:, D + KVW:])
        com['sinks'] = np.asarray(inp['sinks'], f)
        com['bo'] = np.stack([_pp(np.asarray(inp['b_out_odd'], f)[j]) for j in range(cfg.NO)])
    for l in range(DEPTH):
        _split_host(com, f'w_up_{l}', _blk(np.asarray(inp['w_up'], f)[l], 512), 0)
    wd = np.asarray(inp['w_down'], f)
    for l in range(DEPTH):
        for hh in range(cfg.NSP):
            _split_host(com, f'w_down_{l}_{hh}', _blk(wd[l][hh * cfg.HE:(hh + 1) * cfg.HE], cfg.CBD), 0)
    in_maps = []
    for c in range(8):
        bi, ci = c // 4, c % 4
        m = dict(com)
        m['xT'] = np.ascontiguousarray(x[bi, ci * TL:(ci + 1) * TL, :].T)
        cm = np.zeros((128, 8), f)
        for S_ in range(3):
            cm[:, S_] = 0.0 if S_ >= 3 - ci else NEG
        cm[:, 3] = NEG if ci == 0 else 0.0
        cm[:, 4] = 0.0 if ci == 0 else 1.0
        m['cmask'] = cm
        ic = np.zeros((128, 64), f)
        for gi, w in enumerate((2, 4, 8, 16)):
            for t in range(16):
                ic[:, gi * 16 + t] = 1.0 / (min(t + 1, w) if ci == 0 else w)
        m['invcnt'] = ic
        in_maps.append(m)
    return in_maps


def run_cfg(cfg, inp):
    nc = build_program(cfg)
    in_maps = make_inputs(cfg, inp)
    res = run_bass_kernel_spmd(nc, in_maps, core_ids=list(range(8)))
    x = np.asarray(inp['x'])
    out = np.empty(x.shape, np.float32)
    global LAST_RES
    LAST_RES = res.results
    for c in range(8):
        bi, ci = c // 4, c % 4
        out[bi, ci * cfg.TL:(ci + 1) * cfg.TL, :] = res.results[c]['outT'].T
    return out


def kernel(**inputs):
    cfg = Cfg(D=4096, TL=2048, DEPTH=4)
    return run_cfg(cfg, inputs)
```

```python
import math
import numpy as np
import concourse.bass as bass
import concourse.mybir as mybir
from concourse.bass_utils import run_bass_kernel_spmd

F32 = mybir.dt.float32
BF16 = mybir.dt.bfloat16
AF = mybir.ActivationFunctionType
ALU = mybir.AluOpType
AX = mybir.AxisListType
NEG = -30000.0
EPS = 1e-5


class Cfg:
    def __init__(self, D=4096, TL=2048, DEPTH=4):
        self.D = D; self.TL = TL; self.DEPTH = DEPTH
        self.T = 512; self.NT = TL // 512; self.KD = D // 128
        self.PW = D // 2; self.PG = self.PW // 4; self.DW = D - self.PW; self.DH = self.DW // 128
        self.EVEN_IN = self.PW + 3 * self.DW
        self.SH = D // 64; self.KVH = self.SH // 8; self.KVW = self.KVH * 64
        self.DFF = 4 * D; self.HF = self.DFF // 2
        self.NE = (DEPTH + 1) // 2; self.NO = DEPTH // 2
        self.CBK = min(512, 2 * self.KVW)
        self.CBD = 256
        self.WSLOT = max(self.KD * 512, (self.HF // 128) * self.CBD)


class Cnt:
    def __init__(self, nc, name):
        self.h = nc.alloc_semaphore(name); self.v = 0; self.inc = 1


class Ring:
    def __init__(self, b, name, bufs, dma_fill=False):
        self.b = b; self.bufs = bufs; self.depth = len(bufs); self.n = 0
        self.fill = [None] * self.depth
        self.free = [[] for _ in range(self.depth)]
        self.dsem = [b.new_dsem(f"{name}_f{i}") for i in range(self.depth)] if dma_fill else None

    def next(self):
        s = self.n % self.depth; self.n += 1; return s

    def wait_free(self, eng, s):
        for ev in self.free[s]:
            self.b.wait(eng, ev)
        self.free[s] = []


class Builder:
    def __init__(self):
        self.nc = bass.Bass("TRN2", target_bir_lowering=False)
        nc = self.nc
        self.eng = {'pe': nc.tensor, 'act': nc.scalar, 'dve': nc.vector, 'pool': nc.gpsimd, 'sp': nc.sync}
        self.cnt = {e: Cnt(nc, "c_" + e) for e in ('pe', 'act', 'dve', 'pool')}
        self.waited = {}
        self.dsems = []
        self.nd = 0

    def new_dsem(self, name):
        c = Cnt(self.nc, name); c.inc = 16; self.dsems.append(c); return c

    def wait(self, eng, ev):
        if ev is None:
            return
        c, v = ev
        key = (eng, id(c))
        if self.waited.get(key, 0) >= v:
            return
        self.waited[key] = v
        self.eng[eng].wait_ge(c.h, v)

    def mark(self, eng, instr, fence=True):
        c = self.cnt[eng]
        instr.then_inc(c.h, 1); c.v += 1
        self.waited[(eng, id(c))] = c.v
        if eng != 'pe' and fence:
            self.eng[eng].wait_ge(c.h, c.v)
        return (c, c.v)

    def dma(self, q, out, in_, dsem):
        try:
            instr = self.eng[q].dma_start(out=out, in_=in_)
        except Exception:
            print("DMA FAIL", q, out, in_, "nd", self.nd)
            raise
        self.nd += 1
        instr.then_inc(dsem.h, 16); dsem.v += 16
        return (dsem, dsem.v)

    def barrier(self):
        evs = [(c, c.v) for c in self.cnt.values() if c.v > 0] + [(c, c.v) for c in self.dsems if c.v > 0]
        for e in ('sp', 'pool', 'pe', 'act', 'dve'):
            for ev in evs:
                self.wait(e, ev)


def build_program(cfg):
    b = Builder(); nc = b.nc
    D, TL, T, NT, KD = cfg.D, cfg.TL, cfg.T, cfg.NT, cfg.KD
    PW, PG, DW, DH = cfg.PW, cfg.PG, cfg.DW, cfg.DH
    SH, KVH, KVW = cfg.SH, cfg.KVH, cfg.KVW
    DFF, HF, NE, NO, DEPTH = cfg.DFF, cfg.HF, cfg.NE, cfg.NO, cfg.DEPTH
    PGC = PG // 128
    HFC = HF // 128
    CBK, CBD = cfg.CBK, cfg.CBD
    NKB = (2 * KVW) // CBK
    wait, mark, dma = b.wait, b.mark, b.dma
    pe, act, dve, pool, sp = nc.tensor, nc.scalar, nc.vector, nc.gpsimd, nc.sync

    def din(name, shape, dt=F32):
        return nc.dram_tensor(name, list(shape), dt, kind="ExternalInput").ap()

    def dscr(name, shape, dt):
        return nc.dram_tensor(name, list(shape), dt).ap()

    MAXB = 64 * 2 ** 20

    class WSplit:
        def __init__(self, name, lead, nblk, width):
            self.per = max(1, MAXB // (128 * width * 4))
            self.parts = {}
            import itertools
            for idx in itertools.product(*[range(n) for n in lead]):
                lst = []
                for p0 in range(0, nblk, self.per):
                    n_ = min(self.per, nblk - p0)
                    lst.append(din(name + "".join(f"_{i}" for i in idx) + f"_p{p0 // self.per}", [n_, 128, width]))
                self.parts[idx] = lst

        def __getitem__(self, key):
            idx, i = tuple(key[:-1]), key[-1]
            return self.parts[idx][i // self.per][i % self.per]

    xT_in = din("xT", [D, TL])
    outT = nc.dram_tensor("outT", [D, TL], F32, kind="ExternalOutput").ap()
    gvec_d = din("gvec", [128, (2 * DEPTH + 1) * KD])
    w_in_e = WSplit("w_in_e", [NE], cfg.EVEN_IN // 512, KD * 512)
    w_pool = din("w_pool", [NE, 4, 128, PGC * PG])
    w_out_e = WSplit("w_out_e", [NE], D // 512, KD * 512)
    pscale_d = din("pscale", [NE, 128, PW // 128])
    lamv_d = din("lamv", [NE, 4, 64])
    subg_d = din("subg", [NE, 128])
    if NO > 0:
        w_q_o = WSplit("w_q_o", [NO], D // 512, KD * 512)
        w_k_o = din("w_k_o", [NO, NKB, 128, KD * CBK])
        w_v_o = din("w_v_o", [NO, 1, 128, KD * KVW])
        w_out_o = WSplit("w_out_o", [NO], D // 512, KD * 512)
        bq_d = din("bq", [NO, 128, D // 128])
        bk_d = din("bk", [NO, 128, 2 * KVW // 128])
        bv_d = din("bv", [NO, KVW])
        sinks_d = din("sinks", [NO, SH])
        bo_d = din("bo", [NO, 128, KD])
    w_up = WSplit("w_up", [DEPTH], DFF // 512, KD * 512)
    w_down = WSplit("w_down", [DEPTH, 2], D // CBD, HFC * CBD)
    cmask_d = din("cmask", [128, 8])
    invcnt_d = din("invcnt", [128, 64])

    xA = dscr("xA", [D, TL], F32)
    xB = dscr("xB", [D, TL], F32)
    uT_s = dscr("uT_s", [PW, TL], F32)
    qT_s = dscr("qT_s", [D, TL], BF16)
    ocT_s = dscr("ocT_s", [D, TL], BF16)
    K_pay = dscr("K_pay", [DW, TL], BF16)
    K_all = dscr("K_all", [4 * DW, TL], BF16)
    V_pay = dscr("V_pay", [TL, DW], BF16)
    V_all = dscr("V_all", [4 * TL, DW], BF16)
    MR = PW + 8
    misc_pay = dscr("misc_pay", [MR, 16], F32)
    misc_all = dscr("misc_all", [4 * MR, 16], F32)
    Ko_s = dscr("Ko_s", [2 * KVW, TL], BF16)
    Vo_s = dscr("Vo_s", [TL, KVW], BF16)
    OH = 2 * KVW + 8
    ho_pay = dscr("ho_pay", [OH, 128], BF16)
    ho_all = dscr("ho_all", [4 * OH, 128], BF16)
    hv_pay = dscr("hv_pay", [128, KVW], BF16)
    hv_all = dscr("hv_all", [4 * 128, KVW], BF16)
    mx_s = dscr("mx_s", [128, 2], F32)
    mo_pay = dscr("mo_pay", [8, 16], F32)
    mo_all = dscr("mo_all", [32, 16], F32)

    def sb(name, shape, dt):
        return nc.alloc_sbuf_tensor(name, list(shape), dt)

    wbufs = [sb(f"wb{i}", [128, cfg.WSLOT], BF16) for i in range(2)]
    wring = Ring(b, "w", wbufs, dma_fill=True)
    xin = Ring(b, "xin", [sb(f"xin{i}", [128, T], F32) for i in range(4)], dma_fill=True)
    ost = Ring(b, "ost", [sb(f"ost{i}", [128, T], F32) for i in range(2)])
    ost_ds = [b.new_dsem(f"ost_s{i}") for i in range(2)]
    osb = Ring(b, "osb", [sb(f"osb{i}", [128, T], BF16) for i in range(3)])
    osb_ds = [b.new_dsem(f"osb_s{i}") for i in range(3)]
    sq = Ring(b, "sq", [sb(f"sq{i}", [128, T], BF16) for i in range(2)])
    gvec = sb("gvec_sb", [128, (2 * DEPTH + 1) * KD], F32)
    cmask = sb("cmask_sb", [128, 8], F32)
    invcnt = sb("invcnt_sb", [128, 64], F32)
    ident = sb("ident", [128, 128], BF16)
    ones_bf = sb("ones_bf", [128, 128], BF16)
    blk_ones = sb("blk_ones", [128, 128], BF16)
    tri_le = sb("tri_le", [128, 128], BF16)
    tri_gt = sb("tri_gt", [128, 128], BF16)
    tmpf = sb("tmpf", [128, 128], F32)
    rstd = sb("rstd", [128, T], F32)
    rtmp = sb("rtmp", [128, T], F32)
    small = sb("small", [128, 64], F32)
    eps_t = sb("eps_t", [128, 1], F32)
    misc_ds = b.new_dsem("misc_ds")
    cc_sem = Cnt(nc, "cc_sem")

    pb = [nc.alloc_psum_tensor(f"pb{i}", [128, 512], F32) for i in range(7)]
    pbt = nc.alloc_psum_tensor("pbt", [128, 1024], BF16)
    pfree = [[] for _ in range(8)]

    def bank_wait(eng, i):
        for ev in pfree[i]:
            wait(eng, ev)
        pfree[i] = []

    e0 = dma('sp', gvec[:], gvec_d, misc_ds)
    e0 = dma('sp', cmask[:], cmask_d, misc_ds)
    e0 = dma('sp', invcnt[:], invcnt_d, misc_ds)
    mark('pool', pool.memset(tmpf[:], 1.0))
    ev = mark('pool', pool.affine_select(out=tmpf[:], in_=tmpf[:], pattern=[[-1, 128]], compare_op=ALU.is_equal,
                                         fill=0.0, base=0, channel_multiplier=1))
    wait('dve', ev)
    ev = mark('dve', dve.tensor_copy(out=ident[:], in_=tmpf[:]))
    wait('pool', ev)
    mark('pool', pool.memset(tmpf[:], 1.0))
    ev = mark('pool', pool.affine_select(out=tmpf[:], in_=tmpf[:], pattern=[[1, 128]], compare_op=ALU.is_ge,
                                         fill=0.0, base=0, channel_multiplier=-1))
    wait('dve', ev)
    mark('dve', dve.tensor_copy(out=tri_le[:], in_=tmpf[:]))
    mark('dve', dve.tensor_scalar(out=tmpf[:], in0=tmpf[:], scalar1=-1.0, scalar2=1.0, op0=ALU.mult, op1=ALU.add))
    mark('dve', dve.tensor_copy(out=tri_gt[:], in_=tmpf[:]))
    mark('dve', dve.memset(ones_bf[:], 1.0))
    mark('dve', dve.memset(blk_ones[:], 0.0))
    mark('dve', dve.memset(blk_ones[0:64, 0:64], 1.0))
    mark('dve', dve.memset(blk_ones[64:128, 64:128], 1.0))
    mark('dve', dve.memset(eps_t[:], EPS))
    b.barrier()

    def load_w(block_ap, ncols):
        s = wring.next()
        wring.wait_free('pool', s)
        wring.fill[s] = dma('pool', wbufs[s][:, 0:ncols], block_ap, wring.dsem[s])
        return s

    class WStream:
        def __init__(self, blocks):
            self.blocks = blocks; self.issued = 0; self.slots = []

        def get(self, i):
            while self.issued < len(self.blocks) and self.issued <= i + 1 and self.issued - i < 2:
                ap, n = self.blocks[self.issued]
                self.slots.append(load_w(ap, n)); self.issued += 1
            return self.slots[i]

    def norm_tile(xsrc, tt, gcol, hT):
        tsl = slice(tt * T, (tt + 1) * T)
        bank_wait('pe', 6)
        e3 = None
        for c in range(KD):
            s = xin.next(); xin.wait_free('sp', s)
            ev = dma('sp', xin.bufs[s][:], xsrc[c * 128:(c + 1) * 128, tsl], xin.dsem[s])
            q = sq.next(); sq.wait_free('act', q); wait('act', ev)
            e2 = mark('act', act.activation(out=sq.bufs[q][:], in_=xin.bufs[s][:], func=AF.Square), fence=False)
            xin.free[s].append(e2)
            wait('pe', e2)
            e3 = mark('pe', pe.matmul(pb[6][:], lhsT=ones_bf[:], rhs=sq.bufs[q][:], start=(c == 0), stop=(c == KD - 1)))
            sq.free[q].append(e3)
        wait('act', e3)
        e4 = mark('act', act.activation(out=rtmp[:], in_=pb[6][:], func=AF.Sqrt, bias=eps_t[:], scale=1.0 / D))
        pfree[6].append(e4)
        wait('dve', e4)
        e5 = mark('dve', dve.reciprocal(out=rstd[:], in_=rtmp[:]))
        e6 = None
        for c in range(KD):
            s = xin.next(); xin.wait_free('sp', s)
            ev = dma('sp', xin.bufs[s][:], xsrc[c * 128:(c + 1) * 128, tsl], xin.dsem[s])
            wait('dve', ev)
            e6 = mark('dve', dve.scalar_tensor_tensor(out=hT[:, c, :], in0=xin.bufs[s][:], scalar=gvec[:, gcol + c:gcol + c + 1],
                                                      in1=rstd[:], op0=ALU.mult, op1=ALU.mult), fence=(c == KD - 1))
            xin.free[s].append(e6)
        return e6

    lin_bank = [0]

    def lin_fm(inT, KC, in_ev, ws, blk0, nblk, CB, evac, in_free=None):
        last = None
        for bi in range(nblk):
            s = ws.get(blk0 + bi)
            wait('pe', wring.fill[s])
            wait('pe', in_ev)
            for m in range(CB // 128):
                pbi = lin_bank[0] % 4; lin_bank[0] += 1
                bank_wait('pe', pbi)
                for k in range(KC):
                    mm = pe.matmul(pb[pbi][:], lhsT=wbufs[s][:, k * CB + m * 128:k * CB + (m + 1) * 128],
                                   rhs=inT[:, k, :], start=(k == 0), stop=(k == KC - 1))
                rdy = mark('pe', mm)
                fe = evac(bi * (CB // 128) + m, pbi, rdy)
                pfree[pbi].append(fe)
                last = rdy
            wring.free[s].append(last)
        if in_free is not None:
            in_free.append(last)
        return last

    def store(q, dst, ring, ds, s, ev):
        wait(q, ev)
        e = dma(q, dst, ring.bufs[s][:], ds[s])
        ring.free[s].append(e)
        return e

    def evac_store_bf(dst_fn, bias_fn=None, scale_fn=None, eng_alt=True):
        def f(ti, pbi, rdy):
            s = osb.next(); osb.wait_free('act', s)
            wait('act', rdy)
            if bias_fn is not None:
                e = mark('act', act.activation(out=osb.bufs[s][:], in_=pb[pbi][:], func=AF.Identity, bias=bias_fn(ti), scale=1.0), fence=False)
            elif scale_fn is not None:
                e = mark('act', act.activation(out=osb.bufs[s][:], in_=pb[pbi][:], func=AF.Identity, scale=scale_fn(ti)), fence=False)
            else:
                e = mark('act', act.activation(out=osb.bufs[s][:], in_=pb[pbi][:], func=AF.Copy), fence=False)
            store('sp', dst_fn(ti), osb, osb_ds, s, e)
            return e
        return f

    def evac_residual(xsrc, xdst, tt, bias_fn=None):
        tsl = slice(tt * T, (tt + 1) * T)

        def f(ti, pbi, rdy):
            if ti == 0:
                for ds_ in ost_ds:
                    wait('sp', (ds_, ds_.v))
            xs = xin.next(); xin.wait_free('sp', xs)
            ev = dma('sp', xin.bufs[xs][:], xsrc[ti * 128:(ti + 1) * 128, tsl], xin.dsem[xs])
            s = ost.next(); ost.wait_free('dve', s)
            wait('dve', ev); wait('dve', rdy)
            if bias_fn is None:
                e = mark('dve', dve.tensor_tensor(out=ost.bufs[s][:], in0=pb[pbi][:], in1=xin.bufs[xs][:], op=ALU.add), fence=False)
            else:
                e = mark('dve', dve.scalar_tensor_tensor(out=ost.bufs[s][:], in0=pb[pbi][:], scalar=bias_fn(ti),
                                                         in1=xin.bufs[xs][:], op0=ALU.add, op1=ALU.add), fence=False)
            xin.free[xs].append(e)
            store('sp', xdst[ti * 128:(ti + 1) * 128, tsl], ost, ost_ds, s, e)
            return e
        return f

    def blocks_of(wd, idx, nblk, ncols):
        return [(wd[idx + (i,)] if isinstance(idx, tuple) else wd[idx, i], ncols) for i in range(nblk)]

    import contextlib
    hones = [sb("hones0", [128, 128], BF16), sb("hones1", [128, 128], BF16)]
    mrep_le = sb("mrep_le", [128, 1024], BF16)
    mrep_gt = sb("mrep_gt", [128, 1024], BF16)
    biasT = sb("biasT", [128, 16], F32)
    mxc = sb("mxc", [128, 64], F32)
    mxb = sb("mxb", [128, 4], BF16)
    gsub = sb("gsub", [128, 128], F32)
    lamt = sb("lamt", [128, 256], F32)
    zt = sb("zt", [128, 512], BF16)
    mark('dve', dve.memset(zt[:], 0.0))
    for c in range(2):
        mark('dve', dve.memset(hones[c][:], 0.0))
        mark('dve', dve.memset(hones[c][c * 64:(c + 1) * 64, :], 1.0))
    for g in range(8):
        mark('dve', dve.tensor_copy(out=mrep_le[:, g * 128:(g + 1) * 128], in_=tri_le[:]))
        mark('dve', dve.tensor_copy(out=mrep_gt[:, g * 128:(g + 1) * 128], in_=tri_gt[:]))
    b.barrier()
    pid_sp = sp.partition_id()
    rkS = [sp.snap((pid_sp + S_ + 1) % 4) for S_ in range(4)]
    prevr = rkS[2]
    NB = TL // 128
    K4 = K_all.rearrange("(r d) t -> r d t", r=4)
    V4 = V_all.rearrange("(r b p) c -> r p b c", r=4, p=128)
    M4 = misc_all.rearrange("(r m) c -> r m c", r=4)
    HO4 = ho_all.rearrange("(r m) c -> r m c", r=4)
    HV4 = hv_all.rearrange("(r m) c -> r m c", r=4)
    Kslot = dscr("Kslot", [4, DW, TL], BF16)
    Vslot = dscr("Vslot", [4, TL, DW], BF16)
    Mprev = dscr("Mprev", [MR, 16], F32)
    HOprev = dscr("HOprev", [OH, 128], BF16)
    HVprev = dscr("HVprev", [128, KVW], BF16)
    V4f = V_all.rearrange("(r t) c -> r t c", r=4)

    def mlp_phase(l, xsrc, xmid, xdst):
        with contextlib.ExitStack() as st:
            hT = st.enter_context(nc.sbuf_tensor(f"hT_m{l}", [128, KD, T], BF16))
            aT = st.enter_context(nc.sbuf_tensor(f"aT_m{l}", [128, HFC, T], BF16))
            rl = st.enter_context(nc.sbuf_tensor(f"rl_m{l}", [128, 2, T], F32))
            blocks = []
            for tt in range(NT):
                for hh in range(2):
                    blocks += [(w_up[l, hh * (HF // 512) + i], KD * 512) for i in range(HF // 512)]
                    blocks += [(w_down[l, hh, i], HFC * CBD) for i in range(D // CBD)]
            ws = WStream(blocks)
            bi = 0
            hT_free = []
            aT_free = []
            rl_free = [[], []]
            rl_n = [0]
            for tt in range(NT):
                for ev in hT_free:
                    wait('dve', ev)
                hT_free = []
                hev = norm_tile(xsrc, tt, (DEPTH + l) * KD, hT)
                for hh in range(2):
                    a_last = [None]

                    def evac_up(ti, pbi, rdy):
                        r = rl_n[0] % 2; rl_n[0] += 1
                        for ev in rl_free[r]:
                            wait('act', ev)
                        rl_free[r] = []
                        if ti == 0:
                            for ev in aT_free:
                                wait('dve', ev)
                            aT_free.clear()
                        wait('act', rdy)
                        e1 = mark('act', act.activation(out=rl[:, r, :], in_=pb[pbi][:], func=AF.Relu), fence=False)
                        wait('dve', e1)
                        e2 = mark('dve', dve.tensor_tensor(out=aT[:, ti, :], in0=rl[:, r, :], in1=rl[:, r, :], op=ALU.mult), fence=False)
                        rl_free[r].append(e2)
                        a_last[0] = e2
                        return e1
                    lin_fm(hT, KD, hev, ws, bi, HF // 512, 512, evac_up, in_free=(hT_free if hh == 1 else None))
                    bi += HF // 512
                    src = xsrc if hh == 0 else xmid
                    dst = xmid if hh == 0 else xdst
                    lin_fm(aT, HFC, a_last[0], ws, bi, D // CBD, CBD, evac_residual(src, dst, tt), in_free=aT_free)
                    bi += D // CBD
            b.barrier()

    def lin_tm(hT, in_ev, s, ncols, tt, dst, col0, bias_tile=None):
        wait('pe', wring.fill[s]); wait('pe', in_ev)
        last = None
        for tb in range(4):
            pbi = lin_bank[0] % 4; lin_bank[0] += 1
            bank_wait('pe', pbi)
            for k in range(KD):
                mm = pe.matmul(pb[pbi][:, 0:ncols], lhsT=hT[:, k, tb * 128:(tb + 1) * 128],
                               rhs=wbufs[s][:, k * ncols:(k + 1) * ncols], start=(k == 0), stop=(k == KD - 1))
            rdy = mark('pe', mm)
            o = osb.next(); osb.wait_free('dve', o)
            wait('dve', rdy)
            if bias_tile is None:
                e = mark('dve', dve.tensor_copy(out=osb.bufs[o][:, 0:ncols], in_=pb[pbi][:, 0:ncols]), fence=False)
            else:
                e = mark('dve', dve.tensor_tensor(out=osb.bufs[o][:, 0:ncols], in0=pb[pbi][:, 0:ncols], in1=bias_tile, op=ALU.add), fence=False)
            pfree[pbi].append(e)
            wait('sp', e)
            r0 = tt * T + tb * 128
            osb.free[o].append(dma('sp', dst[r0:r0 + 128, col0:col0 + ncols], osb.bufs[o][:, 0:ncols], osb_ds[o]))
            last = rdy
        wring.free[s].append(last)
        return last

    def evac_store_f32(dst_fn):
        def f(ti, pbi, rdy):
            s = ost.next(); ost.wait_free('act', s)
            wait('act', rdy)
            e = mark('act', act.activation(out=ost.bufs[s][:], in_=pb[pbi][:], func=AF.Copy), fence=False)
            store('sp', dst_fn(ti), ost, ost_ds, s, e)
            return e
        return f

    def even_inproj(j, l, xsrc):
        with contextlib.ExitStack() as st:
            hT = st.enter_context(nc.sbuf_tensor(f"hT_e{l}", [128, KD, T], BF16))
            nbk = cfg.EVEN_IN // 512
            ws = WStream([(w_in_e[j, i], KD * 512) for _ in range(NT) for i in range(nbk)])
            hfree = []
            for tt in range(NT):
                tsl = slice(tt * T, (tt + 1) * T)
                for ev in hfree:
                    wait('dve', ev)
                hfree = []
                hev = norm_tile(xsrc, tt, l * KD, hT)
                base = tt * nbk
                nu, nq = PW // 512, DW // 512
                lin_fm(hT, KD, hev, ws, base, nu, 512,
                       evac_store_f32(lambda ti: uT_s[ti * 128:(ti + 1) * 128, tsl]))
                lin_fm(hT, KD, hev, ws, base + nu, nq, 512,
                       evac_store_bf(lambda ti: qT_s[ti * 128:(ti + 1) * 128, tsl]))
                lin_fm(hT, KD, hev, ws, base + nu + nq, nq, 512,
                       evac_store_bf(lambda ti: K_pay[ti * 128:(ti + 1) * 128, tsl]))
                for vb in range(nq):
                    s = ws.get(base + nu + 2 * nq + vb)
                    lastv = lin_tm(hT, hev, s, 512, tt, V_pay, vb * 512)
                hfree.append(lastv)
            b.barrier()

    def gather_even():
        e = dma('pool', misc_pay[0:PW, :], uT_s[:, TL - 16:TL], misc_ds)
        wait('pool', e)
        rg = [[0, 1, 2, 3], [4, 5, 6, 7]]
        CCB = 512 * 1024
        rc = max(1, min(DW, CCB // (TL * 2)))
        rv = max(1, min(TL, CCB // (DW * 2)))
        nck, nvk = DW // rc, TL // rv
        pairs = [(K_pay[ck * rc:(ck + 1) * rc, :], K_all[ck * 4 * rc:(ck + 1) * 4 * rc, :]) for ck in range(nck)]
        pairs += [(V_pay[ck * rv:(ck + 1) * rv, :], V_all[ck * 4 * rv:(ck + 1) * 4 * rv, :]) for ck in range(nvk)]
        pairs += [(misc_pay, misc_all)]
        for (i_, o_) in pairs:
            pool.collective_compute("AllGather", ALU.bypass, replica_groups=rg, ins=[i_], outs=[o_]).then_inc(cc_sem.h, 1)
            cc_sem.v += 1
        wait('pool', (cc_sem, cc_sem.v))
        ep = mark('pool', pool.memset(tmpf[:, 0:1], 0.0))
        wait('sp', ep)
        K5 = K_all.rearrange("(ck r rc) t -> ck r rc t", ck=nck, r=4)
        V5 = V_all.rearrange("(ck r rv) c -> ck r rv c", ck=nvk, r=4)
        for S in range(4):
            dma('sp', Kslot[S].rearrange("(ck rc) t -> ck rc t", ck=nck),
                K5[:, bass.ds(rkS[S], 1), :, :].rearrange("ck o rc t -> ck (o rc) t"), misc_ds)
            dma('sp', Vslot[S].rearrange("(ck rv) c -> ck rv c", ck=nvk),
                V5[:, bass.ds(rkS[S], 1), :, :].rearrange("ck o rv c -> ck (o rv) c"), misc_ds)
        dma('sp', Mprev, M4[bass.ds(prevr, 1), :, :].rearrange("o m c -> (o m) c"), misc_ds)
        b.barrier()

    def pool_phase(j):
        with contextlib.ExitStack() as st:
            ub = [st.enter_context(nc.sbuf_tensor(f"ub{j}_{i}", [128, 16 + T], F32)) for i in range(2)]
            ubr = Ring(b, f"ubr{j}", ub, dma_fill=True)
            pa = st.enter_context(nc.sbuf_tensor(f"pa{j}", [128, 16 + T], F32))
            pc = st.enter_context(nc.sbuf_tensor(f"pc{j}", [128, 16 + T], F32))
            t16 = st.enter_context(nc.sbuf_tensor(f"t16{j}", [128, 16], F32))
            pTs = [st.enter_context(nc.sbuf_tensor(f"pT{j}_{i}", [128, PGC, T], BF16)) for i in range(2)]
            pTr = Ring(b, f"pTr{j}", pTs)
            ws = WStream([(w_pool[j, gi], PGC * PG) for _ in range(NT) for gi in range(4)])
            for tt in range(NT):
                tsl = slice(tt * T, (tt + 1) * T)
                for gi in range(4):
                    wdw = (2, 4, 8, 16)[gi]
                    ps_ = pTr.next(); pTr.wait_free('dve', ps_)
                    lastp = None
                    for cc in range(PGC):
                        c = gi * PGC + cc
                        rows = slice(c * 128, (c + 1) * 128)
                        u = ubr.next(); ubr.wait_free('sp', u)
                        if tt == 0:
                            dma('sp', ub[u][:, 16:], uT_s[rows, tsl], ubr.dsem[u])
                            ev = dma('sp', ub[u][:, 0:16], Mprev[c * 128:(c + 1) * 128, :], ubr.dsem[u])
                        else:
                            ev = dma('sp', ub[u][:, :], uT_s[rows, tt * T - 16:(tt + 1) * T], ubr.dsem[u])
                        wait('dve', ev)
                        if tt == 0:
                            mark('dve', dve.tensor_scalar(out=ub[u][:, 0:16], in0=ub[u][:, 0:16], scalar1=cmask[:, 4:5], scalar2=None, op0=ALU.mult))
                        cur = ub[u]; Wd = 16 + T
                        d = 1; k = 0
                        while d < wdw:
                            nxt = pa if k % 2 == 0 else pc
                            mark('dve', dve.tensor_tensor(out=nxt[:, d:Wd], in0=cur[:, d:Wd], in1=cur[:, 0:Wd - d], op=ALU.add))
                            cur = nxt; d *= 2; k += 1
                        lastp = mark('dve', dve.scalar_tensor_tensor(out=pTs[ps_][:, cc, :], in0=cur[:, 16:], scalar=1.0 / wdw,
                                                                     in1=ub[u][:, 16:], op0=ALU.mult, op1=ALU.subtract))
                        if tt == 0:
                            mark('dve', dve.tensor_tensor(out=t16[:], in0=cur[:, 16:32], in1=invcnt[:, gi * 16:(gi + 1) * 16], op=ALU.mult))
                            lastp = mark('dve', dve.tensor_tensor(out=pTs[ps_][:, cc, 0:16], in0=t16[:], in1=ub[u][:, 16:32], op=ALU.subtract))
                        ubr.free[u].append(lastp)
                    blk = tt * 4 + gi
                    lin_fm(pTs[ps_], PGC, lastp, ws, blk, 1, PG,
                           evac_store_bf(lambda ti, gi=gi: ocT_s[gi * PG + ti * 128:gi * PG + (ti + 1) * 128, tsl],
                                         scale_fn=lambda ti, gi=gi: pscale_sb[:, gi * PGC + ti:gi * PGC + ti + 1]),
                           in_free=pTr.free[ps_])
            b.barrier()

    pscale_sb = sb("pscale_sb", [128, PW // 128], F32)
    scale_d = 64 ** -0.5

    def diffattn_phase(j, l):
        lam_init = 0.8 - 0.6 * math.exp(-0.3 * l)
        e = dma('sp', lamt[:], lamv_d[j].rearrange("a d -> (a d)").partition_broadcast(128), misc_ds)
        e = dma('sp', gsub[:], subg_d[j].partition_broadcast(128), misc_ds)
        wait('dve', e);
        mark('dve', dve.tensor_tensor(out=lamt[:, 0:64], in0=lamt[:, 0:64], in1=lamt[:, 64:128], op=ALU.mult))
        mark('dve', dve.tensor_tensor(out=lamt[:, 128:192], in0=lamt[:, 128:192], in1=lamt[:, 192:256], op=ALU.mult))
        mark('dve', dve.reduce_sum(out=small[:, 0:1], in_=lamt[:, 0:64], axis=AX.X))
        ev = mark('dve', dve.reduce_sum(out=small[:, 1:2], in_=lamt[:, 128:192], axis=AX.X))
        wait('act', ev)
        ev = mark('act', act.activation(out=small[:, 2:4], in_=small[:, 0:2], func=AF.Exp))
        wait('dve', ev)
        mark('dve', dve.tensor_tensor(out=small[:, 4:5], in0=small[:, 2:3], in1=small[:, 3:4], op=ALU.subtract))
        mark('dve', dve.tensor_scalar(out=small[:, 5:6], in0=small[:, 4:5], scalar1=-1.0, scalar2=-lam_init, op0=ALU.mult, op1=ALU.add))
        mark('dve', dve.tensor_scalar(out=gsub[:], in0=gsub[:], scalar1=1.0 - lam_init, scalar2=None, op0=ALU.mult))
        neglam = small[:, 5:6]
        with contextlib.ExitStack() as st:
            Kh = [st.enter_context(nc.sbuf_tensor(f"Kh{j}_{i}", [128, 4, TL], BF16)) for i in range(2)]
            Vh = [st.enter_context(nc.sbuf_tensor(f"Vh{j}_{i}", [128, 4 * NB, 129], BF16)) for i in range(2)]
            Qh = [st.enter_context(nc.sbuf_tensor(f"Qh{j}_{i}", [128, TL], BF16)) for i in range(2)]
            hr = Ring(b, f"hr{j}", [None, None], dma_fill=True)
            Pb = [st.enter_context(nc.sbuf_tensor(f"P{j}_{i}", [128, 512], BF16)) for i in range(3)]
            Pr = Ring(b, f"Pr{j}", Pb)
            t2 = st.enter_context(nc.sbuf_tensor(f"t2_{j}", [128, 128], F32))
            of = st.enter_context(nc.sbuf_tensor(f"of_{j}", [128, 128], F32))
            junk = st.enter_context(nc.sbuf_tensor(f"junk_{j}", [128, 128], F32))
            onb = st.enter_context(nc.sbuf_tensor(f"onb_{j}", [128, 128], BF16))
            sm2 = st.enter_context(nc.sbuf_tensor(f"sm2_{j}", [128, 16], F32))
            for i in range(2):
                evm = mark('dve', dve.memset(Vh[i][:, :, 128:129], 1.0))
            wait('sp', evm)

            def issue_loads(h):
                s = hr.next(); hr.wait_free('sp', s)
                for S in range(4):
                    dma('sp', Kh[s][:, S, :], Kslot[S, h * 128:(h + 1) * 128, :], hr.dsem[s])
                    for b0 in range(0, NB, 8):
                        nb_ = min(8, NB - b0)
                        dma('sp', Vh[s][:, S * NB + b0:S * NB + b0 + nb_, 0:128],
                            Vslot[S, b0 * 128:(b0 + nb_) * 128, h * 128:(h + 1) * 128].rearrange("(b p) c -> p b c", p=128), hr.dsem[s])
                hr.fill[s] = dma('sp', Qh[s][:], qT_s[h * 128:(h + 1) * 128, :], hr.dsem[s])
                return s
            slots = {0: issue_loads(0)}
            sbank = [0]
            o_free = []
            for h in range(DH):
                if h + 1 < DH:
                    slots[h + 1] = issue_loads(h + 1)
                s = slots[h]
                ldev = hr.fill[s]
                ncol = 0
                for (src, n512) in [(Qh[s], TL // 512)] + [(Kh[s][:, S, :], TL // 512) for S in range(4)]:
                    for tq in range(n512):
                        q = sq.next(); sq.wait_free('act', q); wait('act', ldev)
                        e2 = mark('act', act.activation(out=sq.bufs[q][:], in_=src[:, tq * 512:(tq + 1) * 512], func=AF.Square))
                        bank_wait('pe', 6); wait('pe', e2)
                        e3 = mark('pe', pe.matmul(pb[6][:], lhsT=blk_ones[:], rhs=sq.bufs[q][:], start=True, stop=True))
                        sq.free[q].append(e3)
                        wait('dve', e3)
                        e4 = mark('dve', dve.reduce_max(out=mxc[:, ncol:ncol + 1], in_=pb[6][:], axis=AX.X))
                        pfree[6].append(e4)
                        ncol += 1
                nq_ = TL // 512
                mark('dve', dve.reduce_max(out=sm2[:, 0:1], in_=mxc[:, 0:nq_], axis=AX.X))
                mark('dve', dve.reduce_max(out=sm2[:, 1:2], in_=mxc[:, nq_:ncol], axis=AX.X))
                mark('dve', dve.tensor_tensor(out=sm2[:, 2:3], in0=sm2[:, 0:1], in1=sm2[:, 1:2], op=ALU.mult))
                e5 = mark('dve', dve.tensor_copy(out=mxb[:, 0:1], in_=sm2[:, 2:3]))
                bank_wait('pe', 6); wait('pe', e5)
                pe.matmul(pb[6][:, 0:1], lhsT=hones[0][:], rhs=mxb[:, 0:1], start=True, stop=True)
                e6 = mark('pe', pe.matmul(pb[6][:, 1:2], lhsT=hones[1][:], rhs=mxb[:, 0:1], start=True, stop=True))
                wait('act', e6)
                e7 = mark('act', act.activation(out=sm2[:, 4:6], in_=pb[6][:, 0:2], func=AF.Sqrt, scale=1.0 / 64))
                pfree[6].append(e7)
                wait('dve', e7)
                mark('dve', dve.tensor_scalar(out=sm2[:, 6:8], in0=sm2[:, 4:6], scalar1=-scale_d * 1.03, scalar2=-0.05, op0=ALU.mult, op1=ALU.add))
                for c in range(2):
                    for S in range(3):
                        mark('dve', dve.tensor_tensor(out=biasT[:, c * 4 + S:c * 4 + S + 1], in0=sm2[:, 6 + c:7 + c], in1=cmask[:, S:S + 1], op=ALU.add))
                    eb = mark('dve', dve.tensor_copy(out=biasT[:, c * 4 + 3:c * 4 + 4], in_=sm2[:, 6 + c:7 + c]))
                wait('act', eb)
                for i in range(NT):
                    nkb = 3 * NB + (i + 1) * 4
                    for ev in o_free:
                        wait('pe', ev)
                    o_free = []
                    lastpv = None
                    for zb in (3, 4, 5):
                        pe.matmul(pb[zb][:], lhsT=zt[:, 0:128], rhs=zt[:], start=True, stop=True)
                    for kb in range(nkb):
                        S = min(kb // NB, 3)
                        kbl = kb - S * NB
                        diag = (S == 3 and kbl >= 4 * i)
                        jb = kbl - 4 * i
                        for c in range(2):
                            bk = sbank[0] % 3; sbank[0] += 1
                            bank_wait('pe', bk); wait('pe', ldev)
                            es = mark('pe', pe.matmul(pb[bk][:], lhsT=Kh[s][c * 64:(c + 1) * 64, S, kbl * 128:(kbl + 1) * 128],
                                                      rhs=Qh[s][c * 64:(c + 1) * 64, i * 512:(i + 1) * 512], start=True, stop=True))
                            p = Pr.next(); Pr.wait_free('act', p); wait('act', es)
                            ee = mark('act', act.activation(out=Pb[p][:], in_=pb[bk][:], func=AF.Exp,
                                                            bias=biasT[:, c * 4 + S:c * 4 + S + 1], scale=scale_d), fence=False)
                            pfree[bk].append(ee)
                            if diag:
                                wait('dve', ee)
                                ee = mark('dve', dve.tensor_tensor(out=Pb[p][:, jb * 128:(jb + 1) * 128], in0=Pb[p][:, jb * 128:(jb + 1) * 128],
                                                                   in1=tri_le[:], op=ALU.mult))
                            wait('pe', ee)
                            for sbq in range(jb if diag else 0, 4):
                                a = c * 4 + sbq
                                oacc = pb[3 + a // 3][:, (a % 3) * 129:(a % 3) * 129 + 129]
                                mm = pe.matmul(oacc, lhsT=Pb[p][:, sbq * 128:(sbq + 1) * 128], rhs=Vh[s][:, S * NB + kbl, :],
                                               start=False, stop=(diag and jb == sbq))
                            lastpv = mark('pe', mm)
                            Pr.free[p].append(lastpv)
                    wait('dve', lastpv)
                    bank_wait('pe', 7)
                    for sbq in range(4):
                        a1, a2 = sbq, 4 + sbq
                        O1 = pb[3 + a1 // 3][:, (a1 % 3) * 129:(a1 % 3) * 129 + 129]
                        O2 = pb[3 + a2 // 3][:, (a2 % 3) * 129:(a2 % 3) * 129 + 129]
                        mark('dve', dve.memset(sm2[:, 11:12], 0.0))
                        mark('dve', dve.reciprocal(out=sm2[:, 8:9], in_=O1[:, 128:129]))
                        mark('dve', dve.reciprocal(out=sm2[:, 9:10], in_=O2[:, 128:129]))
                        mark('dve', dve.tensor_tensor(out=sm2[:, 10:11], in0=sm2[:, 9:10], in1=neglam, op=ALU.mult))
                        mark('dve', dve.tensor_scalar(out=t2[:], in0=O2[:, 0:128], scalar1=sm2[:, 10:11], scalar2=None, op0=ALU.mult))
                        ev = mark('dve', dve.scalar_tensor_tensor(out=of[:], in0=O1[:, 0:128], scalar=sm2[:, 8:9], in1=t2[:],
                                                                  op0=ALU.mult, op1=ALU.add))
                        if sbq == 3:
                            o_free.append(ev)
                        wait('act', ev)
                        mark('act', act.activation(out=junk[:], in_=of[:], func=AF.Square, accum_out=sm2[:, 11:12]))
                        ev = mark('act', act.activation(out=sm2[:, 12:13], in_=sm2[:, 11:12], func=AF.Sqrt, bias=eps_t[:], scale=1.0 / 128))
                        wait('dve', ev)
                        mark('dve', dve.reciprocal(out=sm2[:, 13:14], in_=sm2[:, 12:13]))
                        ev = mark('dve', dve.scalar_tensor_tensor(out=onb[:], in0=of[:], scalar=sm2[:, 13:14], in1=gsub[:],
                                                                  op0=ALU.mult, op1=ALU.mult))
                        wait('pe', ev)
                        ev = mark('pe', pe.transpose(pbt[:, sbq * 128:(sbq + 1) * 128], onb[:], ident[:]))
                        wait('dve', ev)
                    o = osb.next(); osb.wait_free('act', o)
                    wait('act', ev)
                    ev = mark('act', act.activation(out=osb.bufs[o][:], in_=pbt[:, 0:512], func=AF.Copy))
                    pfree[7].append(ev)
                    store('sp', ocT_s[PW + h * 128:PW + (h + 1) * 128, i * 512:(i + 1) * 512], osb, osb_ds, o, ev)
                hr.free[s].append(lastpv)
            b.barrier()

    def outproj_phase(wd, j, xsrc, xdst, bias_sb=None):
        with contextlib.ExitStack() as st:
            its = [st.enter_context(nc.sbuf_tensor(f"oin{id(wd) % 997}_{j}_{i}", [128, KD, T], BF16)) for i in range(2)]
            ir = Ring(b, f"ir{id(wd) % 997}_{j}", its, dma_fill=True)
            ws = WStream([(wd[j, i], KD * 512) for _ in range(NT) for i in range(D // 512)])
            for tt in range(NT):
                tsl = slice(tt * T, (tt + 1) * T)
                s = ir.next(); ir.wait_free('sp', s)
                ev = dma('sp', its[s][:], ocT_s[:, tsl].rearrange("(k p) t -> p k t", p=128), ir.dsem[s])
                bf = (lambda ti: bias_sb[:, ti:ti + 1]) if bias_sb is not None else None
                lin_fm(its[s], KD, ev, ws, tt * (D // 512), D // 512, 512, evac_residual(xsrc, xdst, tt, bias_fn=bf),
                       in_free=ir.free[s])
            b.barrier()

    if NO > 0:
        bq_sb = sb("bq_sb", [128, D // 128], F32)
        bk_sb = sb("bk_sb", [128, 2 * KVW // 128], F32)
        bv_sb = sb("bv_sb", [128, KVW], F32)
        bo_sb = sb("bo_sb", [128, KD], F32)
        sink_sb = sb("sink_sb", [128, SH], F32)
        exps = sb("exps", [128, 8], F32)

    def odd_inproj(j, l, xsrc):
        dma('sp', bq_sb[:], bq_d[j], misc_ds)
        dma('sp', bk_sb[:], bk_d[j], misc_ds)
        dma('sp', bo_sb[:], bo_d[j], misc_ds)
        dma('sp', sink_sb[:], sinks_d[j].partition_broadcast(128), misc_ds)
        e = dma('sp', bv_sb[:], bv_d[j].partition_broadcast(128), misc_ds)
        for q_ in ('act', 'dve'):
            wait(q_, e)
        with contextlib.ExitStack() as st:
            hT = st.enter_context(nc.sbuf_tensor(f"hT_o{l}", [128, KD, T], BF16))
            nqb = D // 512
            blocks = []
            for _ in range(NT):
                blocks += [(w_q_o[j, i], KD * 512) for i in range(nqb)]
                blocks += [(w_k_o[j, i], KD * CBK) for i in range(NKB)]
                blocks += [(w_v_o[j, 0], KD * KVW)]
            ws = WStream(blocks)
            per = nqb + NKB + 1
            hfree = []
            for tt in range(NT):
                tsl = slice(tt * T, (tt + 1) * T)
                for ev in hfree:
                    wait('dve', ev)
                hfree = []
                hev = norm_tile(xsrc, tt, l * KD, hT)
                base = tt * per
                lin_fm(hT, KD, hev, ws, base, nqb, 512,
                       evac_store_bf(lambda ti: qT_s[ti * 128:(ti + 1) * 128, tsl], bias_fn=lambda ti: bq_sb[:, ti:ti + 1]))
                lin_fm(hT, KD, hev, ws, base + nqb, NKB, CBK,
                       evac_store_bf(lambda ti: Ko_s[ti * 128:(ti + 1) * 128, tsl], bias_fn=lambda ti: bk_sb[:, ti:ti + 1]))
                s = ws.get(base + nqb + NKB)
                lastv = lin_tm(hT, hev, s, KVW, tt, Vo_s, 0, bias_tile=bv_sb[:])
                hfree.append(lastv)
            b.barrier()

    def gather_odd():
        dma('pool', ho_pay[0:2 * KVW, :], Ko_s[:, TL - 128:TL], misc_ds)
        e = dma('pool', hv_pay, Vo_s[TL - 128:TL, :], misc_ds)
        wait('pool', e)
        rg = [[0, 1, 2, 3], [4, 5, 6, 7]]
        for (i_, o_) in ((ho_pay, ho_all), (hv_pay, hv_all)):
            pool.collective_compute("AllGather", ALU.bypass, replica_groups=rg, ins=[i_], outs=[o_]).then_inc(cc_sem.h, 1)
            cc_sem.v += 1
        wait('pool', (cc_sem, cc_sem.v))
        ep = mark('pool', pool.memset(tmpf[:, 0:1], 0.0))
        wait('sp', ep)
        dma('sp', HOprev, HO4[bass.ds(prevr, 1), :, :].rearrange("o m c -> (o m) c"), misc_ds)
        dma('sp', HVprev, HV4[bass.ds(prevr, 1), :, :].rearrange("o m c -> (o m) c"), misc_ds)
        b.barrier()

    def swa_phase(j):
        with contextlib.ExitStack() as st:
            Kd = [st.enter_context(nc.sbuf_tensor(f"Kd{j}_{i}", [128, 128 + TL], BF16)) for i in range(2)]
            Vd = [st.enter_context(nc.sbuf_tensor(f"Vd{j}_{i}", [128, NB + 1, 65], BF16)) for i in range(2)]
            Qo = [st.enter_context(nc.sbuf_tensor(f"Qo{j}_{i}", [128, 4, TL], BF16)) for i in range(2)]
            hr = Ring(b, f"hro{j}", [None, None], dma_fill=True)
            Pb = [st.enter_context(nc.sbuf_tensor(f"Po{j}_{i}", [128, 1024], BF16)) for i in range(3)]
            Pr = Ring(b, f"Pro{j}", Pb)
            onb = st.enter_context(nc.sbuf_tensor(f"onbo_{j}", [128, 512], BF16))
            sm2 = st.enter_context(nc.sbuf_tensor(f"sm2o_{j}", [128, 32], F32))
            for i in range(2):
                evm = mark('dve', dve.memset(Vd[i][:, :, 64:65], 1.0))
            wait('sp', evm)

            def issue_loads(kv):
                s = hr.next(); hr.wait_free('sp', s)
                dma('sp', Kd[s][:, 128:], Ko_s[kv * 128:(kv + 1) * 128, :], hr.dsem[s])
                dma('sp', Kd[s][:, 0:128], HOprev[kv * 128:(kv + 1) * 128, :], hr.dsem[s])
                for b0 in range(0, NB, 8):
                    nb_ = min(8, NB - b0)
                    dma('sp', Vd[s][:, 1 + b0:1 + b0 + nb_, 0:64],
                        Vo_s[b0 * 128:(b0 + nb_) * 128, kv * 64:(kv + 1) * 64].rearrange("(b p) c -> p b c", p=128), hr.dsem[s])
                dma('sp', Vd[s][:, 0, 0:64], HVprev[:, kv * 64:(kv + 1) * 64], hr.dsem[s])
                hr.fill[s] = dma('sp', Qo[s][:], qT_s[kv * 512:(kv + 1) * 512, :].rearrange("(t p) n -> p t n", p=128), hr.dsem[s])
                return s
            slots = {0: issue_loads(0)}
            sb2 = [0]
            o_free = []
            KS = int(os.environ.get("KSWA", "99"))
            if KS <= 1:
                b.barrier(); return
            for kv in range(KVH):
                if kv + 1 < KVH:
                    slots[kv + 1] = issue_loads(kv + 1)
                s = slots[kv]
                ldev = hr.fill[s]
                ncol = 0
                srcs = [(Qo[s][:, t, :], TL // 512) for t in range(4)] + [(Kd[s][:, 128:], TL // 512)]
                for (src, n512) in srcs:
                    for tq in range(n512):
                        q = sq.next(); sq.wait_free('act', q); wait('act', ldev)
                        e2 = mark('act', act.activation(out=sq.bufs[q][:], in_=src[:, tq * 512:(tq + 1) * 512], func=AF.Square))
                        bank_wait('pe', 6); wait('pe', e2)
                        e3 = mark('pe', pe.matmul(pb[6][:], lhsT=blk_ones[:], rhs=sq.bufs[q][:], start=True, stop=True))
                        sq.free[q].append(e3)
                        wait('dve', e3)
                        e4 = mark('dve', dve.reduce_max(out=mxc[:, ncol:ncol + 1], in_=pb[6][:], axis=AX.X))
                        pfree[6].append(e4)
                        ncol += 1
                q = sq.next(); sq.wait_free('act', q)
                e2 = mark('act', act.activation(out=sq.bufs[q][:, 0:128], in_=Kd[s][:, 0:128], func=AF.Square))
                bank_wait('pe', 6); wait('pe', e2)
                e3 = mark('pe', pe.matmul(pb[6][:, 0:128], lhsT=blk_ones[:], rhs=sq.bufs[q][:, 0:128], start=True, stop=True))
                sq.free[q].append(e3)
                wait('dve', e3)
                e4 = mark('dve', dve.reduce_max(out=mxc[:, ncol:ncol + 1], in_=pb[6][:, 0:128], axis=AX.X))
                pfree[6].append(e4)
                ncol += 1
                nq_ = 4 * (TL // 512)
                mark('dve', dve.reduce_max(out=sm2[:, 0:1], in_=mxc[:, 0:nq_], axis=AX.X))
                mark('dve', dve.reduce_max(out=sm2[:, 1:2], in_=mxc[:, nq_:ncol], axis=AX.X))
                e5 = mark('dve', dve.tensor_copy(out=mxb[:, 0:2], in_=sm2[:, 0:2]))
                bank_wait('pe', 6); wait('pe', e5)
                e6 = mark('pe', pe.matmul(pb[6][:, 0:2], lhsT=ones_bf[:], rhs=mxb[:, 0:2], start=True, stop=True))
                wait('dve', e6)
                e7 = mark('dve', dve.tensor_copy(out=sm2[:, 6:8], in_=pb[6][:, 0:2]))
                pfree[6].append(e7)
                e7 = mark('dve', dve.tensor_tensor(out=sm2[:, 2:3], in0=sm2[:, 6:7], in1=sm2[:, 7:8], op=ALU.mult))
                wait('act', e7)
                e8 = mark('act', act.activation(out=sm2[:, 3:4], in_=sm2[:, 2:3], func=AF.Sqrt, scale=1.0 / (64 * 128)))
                wait('dve', e8)
                mark('dve', dve.tensor_scalar(out=sm2[:, 4:5], in0=sm2[:, 3:4], scalar1=-scale_d * 1.03, scalar2=-0.05, op0=ALU.mult, op1=ALU.add))
                eb = mark('dve', dve.tensor_tensor(out=sm2[:, 5:6], in0=sm2[:, 4:5], in1=cmask[:, 3:4], op=ALU.add))
                wait('act', eb)
                ex = mark('act', act.activation(out=exps[:], in_=sink_sb[:, kv * 8:(kv + 1) * 8], func=AF.Exp, bias=sm2[:, 4:5], scale=1.0))
                if KS <= 2:
                    b.barrier(); return
                for qb in range(NB):
                    for ev in o_free:
                        wait('pe', ev)
                    o_free = []
                    lastpv = None
                    for zb in (4, 5):
                        pe.matmul(pb[zb][:], lhsT=zt[:, 0:128], rhs=zt[:], start=True, stop=True)
                    for kbi in range(2):
                        kc0 = (qb + kbi) * 128
                        vblk = qb + kbi
                        mrep = mrep_gt if kbi == 0 else mrep_le
                        bias_ap = (sm2[:, 5:6] if qb == 0 else sm2[:, 4:5]) if kbi == 0 else sm2[:, 4:5]
                        bk0 = (sb2[0] % 2) * 2; sb2[0] += 1
                        bank_wait('pe', bk0); bank_wait('pe', bk0 + 1); wait('pe', ldev)
                        for g in range(8):
                            t_, half = g // 2, g % 2
                            mm = pe.matmul(pb[bk0 + half][:, t_ * 128:(t_ + 1) * 128],
                                           lhsT=Kd[s][half * 64:(half + 1) * 64, kc0:kc0 + 128],
                                           rhs=Qo[s][half * 64:(half + 1) * 64, t_, qb * 128:(qb + 1) * 128], start=True, stop=True)
                        es = mark('pe', mm)
                        KSUB = int(os.environ.get("KSUB", "9"))
                        if KSUB <= 0:
                            pfree[bk0].append(es); continue
                        p = Pr.next(); Pr.wait_free('act', p); wait('act', es)
                        e1 = mark('act', act.activation(out=Pb[p][:, 0:512], in_=pb[bk0][:], func=AF.Exp, bias=bias_ap, scale=scale_d), fence=False)
                        pfree[bk0].append(e1)
                        e2 = mark('act', act.activation(out=Pb[p][:, 512:1024], in_=pb[bk0 + 1][:], func=AF.Exp, bias=bias_ap, scale=scale_d), fence=False)
                        pfree[bk0 + 1].append(e2)
                        if KSUB <= 1:
                            continue
                        wait('dve', e2)
                        e3 = mark('dve', dve.tensor_tensor(out=Pb[p][:], in0=Pb[p][:], in1=mrep[:], op=ALU.mult))
                        wait('pe', e3)
                        if KS <= 3:
                            continue
                        for g in range(8):
                            pix = (g % 2) * 4 + g // 2
                            mm = pe.matmul(pb[4 + g // 4][:, (g % 4) * 65:(g % 4) * 65 + 65], lhsT=Pb[p][:, pix * 128:(pix + 1) * 128],
                                           rhs=Vd[s][:, vblk, :], start=False, stop=(kbi == 1))
                        lastpv = mark('pe', mm)
                        Pr.free[p].append(lastpv)
                    if KS <= 4:
                        b.barrier(); return
                    wait('dve', lastpv); wait('dve', ex)
                    bank_wait('pe', 7)
                    for g in range(8):
                        O = pb[4 + g // 4][:, (g % 4) * 65:(g % 4) * 65 + 65]
                        mark('dve', dve.tensor_tensor(out=sm2[:, 8 + g:9 + g], in0=O[:, 64:65], in1=exps[:, g:g + 1], op=ALU.add))
                        mark('dve', dve.reciprocal(out=sm2[:, 16 + g:17 + g], in_=sm2[:, 8 + g:9 + g]))
                        ev = mark('dve', dve.tensor_scalar(out=onb[:, g * 64:(g + 1) * 64], in0=O[:, 0:64], scalar1=sm2[:, 16 + g:17 + g],
                                                           scalar2=None, op0=ALU.mult))
                    o_free.append(ev)
                    if KS <= 5:
                        b.barrier(); return
                    wait('pe', ev)
                    for t_ in range(4):
                        ev = mark('pe', pe.transpose(pbt[:, t_ * 128:(t_ + 1) * 128], onb[:, t_ * 128:(t_ + 1) * 128], ident[:]))
                    wait('dve', ev)
                    o = osb.next(); osb.wait_free('act', o)
                    wait('act', ev)
                    ev = mark('act', act.activation(out=osb.bufs[o][:], in_=pbt[:, 0:512], func=AF.Copy))
                    pfree[7].append(ev)
                    wait('sp', ev)
                    osb.free[o].append(dma('sp', ocT_s[kv * 512:(kv + 1) * 512, qb * 128:(qb + 1) * 128].rearrange("(t p) n -> p t n", p=128),
                                           osb.bufs[o][:].rearrange("p (t n) -> p t n", t=4), osb_ds[o]))
                    if KS <= 6:
                        b.barrier(); return
                hr.free[s].append(lastpv)
            b.barrier()

    def final_norm(xsrc):
        gcol = 2 * DEPTH * KD
        for tt in range(NT):
            tsl = slice(tt * T, (tt + 1) * T)
            bank_wait('pe', 6)
            for c in range(KD):
                s = xin.next(); xin.wait_free('sp', s)
                ev = dma('sp', xin.bufs[s][:], xsrc[c * 128:(c + 1) * 128, tsl], xin.dsem[s])
                q = sq.next(); sq.wait_free('act', q); wait('act', ev)
                e2 = mark('act', act.activation(out=sq.bufs[q][:], in_=xin.bufs[s][:], func=AF.Square), fence=False)
                xin.free[s].append(e2)
                wait('pe', e2)
                e3 = mark('pe', pe.matmul(pb[6][:], lhsT=ones_bf[:], rhs=sq.bufs[q][:], start=(c == 0), stop=(c == KD - 1)))
                sq.free[q].append(e3)
            wait('act', e3)
            e4 = mark('act', act.activation(out=rtmp[:], in_=pb[6][:], func=AF.Sqrt, bias=eps_t[:], scale=1.0 / D))
            pfree[6].append(e4)
            wait('dve', e4)
            mark('dve', dve.reciprocal(out=rstd[:], in_=rtmp[:]))
            for c in range(KD):
                s = xin.next(); xin.wait_free('sp', s)
                ev = dma('sp', xin.bufs[s][:], xsrc[c * 128:(c + 1) * 128, tsl], xin.dsem[s])
                o = ost.next(); ost.wait_free('dve', o)
                wait('dve', ev)
                e6 = mark('dve', dve.scalar_tensor_tensor(out=ost.bufs[o][:], in0=xin.bufs[s][:], scalar=gvec[:, gcol + c:gcol + c + 1],
                                                          in1=rstd[:], op0=ALU.mult, op1=ALU.mult), fence=False)
                xin.free[s].append(e6)
                store('sp', outT[c * 128:(c + 1) * 128, tsl], ost, ost_ds, o, e6)
        b.barrier()

    import os
    KP = int(os.environ.get("KPHASES", "9999"))
    pcount = [0]

    def go():
        pcount[0] += 1
        return pcount[0] <= KP
    bufs = [xA, xB]
    cur = xT_in
    nxt_i = 0
    for l in range(DEPTH):
        j = l // 2
        mixo = bufs[nxt_i]; nxt_i ^= 1
        if l % 2 == 0:
            e = dma('sp', pscale_sb[:], pscale_d[j], misc_ds)
            wait('act', e)
            if go(): even_inproj(j, l, cur)
            if go(): gather_even()
            if go(): pool_phase(j)
            if go(): diffattn_phase(j, l)
            if go():
                outproj_phase(w_out_e, j, cur, mixo)
                cur = mixo
        else:
            if go(): odd_inproj(j, l, cur)
            if go(): gather_odd()
            if go(): swa_phase(j)
            if go():
                outproj_phase(w_out_o, j, cur, mixo, bias_sb=bo_sb)
                cur = mixo
        mid = bufs[nxt_i]
        if go():
            mlp_phase(l, mixo, mid, mixo)
            cur = mixo
    final_norm(cur)
    if os.environ.get("KDEBUG"):
        for nm, src, shp, dt_ in (("dbg_oc", ocT_s, [D, TL], BF16), ("dbg_q", qT_s, [D, TL], BF16), ("dbg_u", uT_s, [PW, TL], F32),
                                  ("dbg_x", xA, [D, TL], F32)):
            dd = nc.dram_tensor(nm, shp, dt_, kind="ExternalOutput").ap()
            e = dma('sp', dd, src, misc_ds)
        wait('sp', e)
    return nc


def _blk(w, CB):
    K, N = w.shape
    return np.ascontiguousarray(w.reshape(K // 128, 128, N // CB, CB).transpose(2, 1, 0, 3)).reshape(N // CB, 128, (K // 128) * CB)


def _pp(v):
    return np.ascontiguousarray(v.reshape(-1, 128).T)


def _split_host(com, name, arr, nlead):
    import itertools
    lead = arr.shape[:nlead]
    nblk, _, width = arr.shape[nlead:]
    per = max(1, (64 * 2 ** 20) // (128 * width * 4))
    for idx in itertools.product(*[range(n) for n in lead]):
        a = arr[idx]
        for p0 in range(0, nblk, per):
            com[name + "".join(f"_{i}" for i in idx) + f"_p{p0 // per}"] = np.ascontiguousarray(a[p0:p0 + per])


def make_inputs(cfg, inp):
    D, TL, DEPTH = cfg.D, cfg.TL, cfg.DEPTH
    PW, DW, KVW, KVH = cfg.PW, cfg.DW, cfg.KVW, cfg.KVH
    f = np.float32
    x = np.asarray(inp['x'], f)
    Bn, S, _ = x.shape
    com = {}
    g = [_pp(np.asarray(inp['norm_mix'], f)[l]) for l in range(DEPTH)] + [_pp(np.asarray(inp['norm_mlp'], f)[l]) for l in range(DEPTH)] \
        + [_pp(np.asarray(inp['norm_final'], f))]
    com['gvec'] = np.ascontiguousarray(np.concatenate(g, axis=1))
    wie = np.asarray(inp['w_in_even'], f)
    _split_host(com, 'w_in_e', np.stack([_blk(wie[j], 512) for j in range(cfg.NE)]), 1)
    wp = np.asarray(inp['w_pool'], f)
    com['w_pool'] = np.stack([np.stack([_blk(wp[j, gi], cfg.PG)[0] for gi in range(4)]) for j in range(cfg.NE)])
    _split_host(com, 'w_out_e', np.stack([_blk(np.asarray(inp['w_out_even'], f)[j], 512) for j in range(cfg.NE)]), 1)
    com['pscale'] = np.stack([_pp(np.asarray(inp['pool_scale'], f)[j]) for j in range(cfg.NE)])
    com['lamv'] = np.ascontiguousarray(np.stack([np.asarray(inp[k], f) for k in ('lambda_q1', 'lambda_k1', 'lambda_q2', 'lambda_k2')], axis=1))
    com['subg'] = np.asarray(inp['subln_g'], f)
    if cfg.NO > 0:
        wio = np.asarray(inp['w_in_odd'], f); bio = np.asarray(inp['b_in_odd'], f)
        _split_host(com, 'w_q_o', np.stack([_blk(wio[j][:, :D], 512) for j in range(cfg.NO)]), 1)
        dup = np.concatenate([np.arange(D + h * 64, D + (h + 1) * 64) for h in range(KVH) for _ in range(2)])
        com['w_k_o'] = np.stack([_blk(wio[j][:, dup], cfg.CBK) for j in range(cfg.NO)])
        com['w_v_o'] = np.stack([_blk(wio[j][:, D + KVW:], KVW) for j in range(cfg.NO)])
        _split_host(com, 'w_out_o', np.stack([_blk(np.asarray(inp['w_out_odd'], f)[j], 512) for j in range(cfg.NO)]), 1)
        com['bq'] = np.stack([_pp(bio[j][:D]) for j in range(cfg.NO)])
        com['bk'] = np.stack([_pp(bio[j][dup]) for j in range(cfg.NO)])
        com['bv'] = np.ascontiguousarray(bio[:, D + KVW:])
        com['sinks'] = np.asarray(inp['sinks'], f)
        com['bo'] = np.stack([_pp(np.asarray(inp['b_out_odd'], f)[j]) for j in range(cfg.NO)])
    for l in range(DEPTH):
        _split_host(com, f'w_up_{l}', _blk(np.asarray(inp['w_up'], f)[l], 512), 0)
    wd = np.asarray(inp['w_down'], f)
    for l in range(DEPTH):
        for hh in range(2):
            _split_host(com, f'w_down_{l}_{hh}', _blk(wd[l][hh * cfg.HF:(hh + 1) * cfg.HF], cfg.CBD), 0)
    in_maps = []
    for c in range(8):
        bi, ci = c // 4, c % 4
        m = dict(com)
        m['xT'] = np.ascontiguousarray(x[bi, ci * TL:(ci + 1) * TL, :].T)
        cm = np.zeros((128, 8), f)
        for S_ in range(3):
            cm[:, S_] = 0.0 if S_ >= 3 - ci else NEG
        cm[:, 3] = NEG if ci == 0 else 0.0
        cm[:, 4] = 0.0 if ci == 0 else 1.0
        m['cmask'] = cm
        ic = np.zeros((128, 64), f)
        for gi, w in enumerate((2, 4, 8, 16)):
            for t in range(16):
                ic[:, gi * 16 + t] = 1.0 / (min(t + 1, w) if ci == 0 else w)
        m['invcnt'] = ic
        in_maps.append(m)
    return in_maps


def run_cfg(cfg, inp):
    nc = build_program(cfg)
    in_maps = make_inputs(cfg, inp)
    res = run_bass_kernel_spmd(nc, in_maps, core_ids=list(range(8)))
    x = np.asarray(inp['x'])
    out = np.empty(x.shape, np.float32)
    global LAST_RES
    LAST_RES = res.results
    for c in range(8):
        bi, ci = c // 4, c % 4
        out[bi, ci * cfg.TL:(ci + 1) * cfg.TL, :] = res.results[c]['outT'].T
    return out


def kernel(**inputs):
    cfg = Cfg(D=4096, TL=2048, DEPTH=4)
    return run_cfg(cfg, inputs)
```
